# Optimizing a Trainium2 kernel written in Bass

```python
import numpy as np
import jax
import jax.numpy as jnp
from jax import lax

D_MODEL = 2048
BATCH = 4
SEQ = 2048
DEPTH = 4
DEC_BATCH = 128
DEC_SEQ = 1
PAST_LEN = 16384
PAGE_SIZE = 128

BRANCH_W = D_MODEL // 2
N_BRANCH = 3
DK_A = 128
DV_A = 128
HA = BRANCH_W // DV_A
CONV_A = 4
CONV_CH_A = 2 * HA * DK_A + HA * DV_A
CHUNK_A = 64
DK_B = 128
DV_B = 256
HB = BRANCH_W // DV_B
CHUNK_B = 64
ROPE_BASE = 10000.0
DK_C = 128
DV_C = 256
HC = BRANCH_W // DV_C
GLA_RANK = 16
GLA_TAU = 16.0
CHUNK_C = 16
D_FF = ((8 * D_MODEL // 3 + 255) // 256) * 256
FFN_CONV = 3
EPS = 1e-6

IN_SIZES = (CONV_CH_A, HA * DV_A, HA, HA,
            HB * DK_B, HB * DK_B, HB * DV_B, HB * DV_B,
            HC * DK_C, HC * DK_C, HC * DV_C, HC * DV_C, GLA_RANK,
            N_BRANCH * D_MODEL)
D_IN = sum(IN_SIZES)
SPLIT_POINTS = tuple(int(s) for s in np.cumsum(IN_SIZES)[:-1])

kernel_name = "hybrid_deltanet_retnet_gla_convffn_decode_step"


def _rmsnorm(x, g=None):
    xf = x.astype(jnp.float32)
    y = xf * lax.rsqrt(jnp.mean(xf * xf, axis=-1, keepdims=True) + EPS)
    return y if g is None else y * g.astype(jnp.float32)


def _l2norm(x):
    return x * lax.rsqrt(jnp.sum(x * x, axis=-1, keepdims=True) + EPS)


def _tril(c, k):
    return jnp.tril(jnp.ones((c, c), dtype=bool), k)


def _causal_dwconv(x, buf, w):
    n_tap = w.shape[0]
    length = x.shape[1]
    xp = jnp.concatenate([buf.astype(x.dtype), x], axis=1)
    y = xp[:, 0:length] * w[0]
    for i in range(1, n_tap):
        y = y + xp[:, i:i + length] * w[i]
    return y, xp[:, xp.shape[1] - (n_tap - 1):]


def _rope(x, pos):
    half = x.shape[-1] // 2
    inv = ROPE_BASE ** (-jnp.arange(half, dtype=jnp.float32) / half)
    ang = pos[:, None] * inv[None, :]
    cos = jnp.cos(ang)[None, :, None, :]
    sin = jnp.sin(ang)[None, :, None, :]
    x1, x2 = x[..., :half], x[..., half:]
    return jnp.concatenate([x1 * cos - x2 * sin, x2 * cos + x1 * sin], axis=-1)


def _to_chunks(x, chunk):
    bsz, length = x.shape[0], x.shape[1]
    pad = (-length) % chunk
    x = jnp.pad(x, [(0, 0), (0, pad)] + [(0, 0)] * (x.ndim - 2))
    n = (length + pad) // chunk
    x = x.reshape((bsz, n, chunk) + x.shape[2:])
    return x.transpose((1, 0, 3, 2) + tuple(range(4, x.ndim)))


def _from_chunks(o, length):
    n, bsz, nh, c, dv = o.shape
    return o.transpose(1, 0, 3, 2, 4).reshape(bsz, n * c, nh, dv)[:, :length]


def _chunk_gated_delta(q, k, v, beta, g, s0, chunk):
    length = q.shape[1]
    dv = v.shape[-1]
    q, k, v, beta, g = (_to_chunks(t, chunk) for t in (q, k, v, beta, g))
    b = jnp.cumsum(g, axis=-1)
    diff = b[..., :, None] - b[..., None, :]
    d_strict = jnp.exp(jnp.where(_tril(chunk, -1), diff, -jnp.inf))
    d_incl = jnp.exp(jnp.where(_tril(chunk, 0), diff, -jnp.inf))
    a_mat = jnp.eye(chunk, dtype=q.dtype) + beta[..., :, None] * jnp.einsum('nbhtd,nbhjd->nbhtj', k, k) * d_strict
    rhs = jnp.concatenate([beta[..., None] * v, (beta * jnp.exp(b))[..., None] * k], axis=-1)
    sol = lax.linalg.triangular_solve(a_mat, rhs, left_side=True, lower=True, unit_diagonal=True)
    u_v, w = sol[..., :dv], sol[..., dv:]
    qk = jnp.einsum('nbhtd,nbhjd->nbhtj', q, k) * d_incl
    q_in = q * jnp.exp(b)[..., None]
    k_out = k * jnp.exp(b[..., -1:] - b)[..., None]
    cd = jnp.exp(b[..., -1])

    def step(s, xs):
        uv_c, w_c, qk_c, qi_c, ko_c, cd_c = xs
        u = uv_c - jnp.einsum('bhtd,bhdv->bhtv', w_c, s)
        o = jnp.einsum('bhtd,bhdv->bhtv', qi_c, s) + jnp.einsum('bhtj,bhjv->bhtv', qk_c, u)
        s = s * cd_c[..., None, None] + jnp.einsum('bhtd,bhtv->bhdv', ko_c, u)
        return s, o

    s, o = lax.scan(step, s0, (u_v, w, qk, q_in, k_out, cd))
    return _from_chunks(o, length), s


def _chunk_scalar_decay(q, k, v, g, s0, chunk):
    length = q.shape[1]
    q, k, v, g = (_to_chunks(t, chunk) for t in (q, k, v, g))
    b = jnp.cumsum(g, axis=-1)
    dec = jnp.exp(jnp.where(_tril(chunk, 0), b[..., :, None] - b[..., None, :], -jnp.inf))
    o_intra = jnp.einsum('nbhtj,nbhjv->nbhtv', jnp.einsum('nbhtd,nbhjd->nbhtj', q, k) * dec, v)
    q_in = q * jnp.exp(b)[..., None]
    k_out = k * jnp.exp(b[..., -1:] - b)[..., None]
    cd = jnp.exp(b[..., -1])

    def step(s, xs):
        qi_c, ko_c, v_c, cd_c, oi_c = xs
        o = oi_c + jnp.einsum('bhtd,bhdv->bhtv', qi_c, s)
        s = s * cd_c[..., None, None] + jnp.einsum('bhtd,bhtv->bhdv', ko_c, v_c)
        return s, o

    s, o = lax.scan(step, s0, (q_in, k_out, v, cd, o_intra))
    return _from_chunks(o, length), s


def _chunk_gla(q, k, v, g, s0, chunk):
    length = q.shape[1]
    q, k, v, g = (_to_chunks(t, chunk) for t in (q, k, v, g))
    b = jnp.cumsum(g, axis=-2)
    diff = b[..., :, None, :] - b[..., None, :, :]
    dec = jnp.exp(jnp.where(_tril(chunk, 0)[:, :, None], diff, -jnp.inf))
    attn = jnp.einsum('nbhtd,nbhjd,nbhtjd->nbhtj', q, k, dec)
    o_intra = jnp.einsum('nbhtj,nbhjv->nbhtv', attn, v)
    q_in = q * jnp.exp(b)
    k_out = k * jnp.exp(b[..., -1:, :] - b)
    cd = jnp.exp(b[..., -1, :])

    def step(s, xs):
        qi_c, ko_c, v_c, cd_c, oi_c = xs
        o = oi_c + jnp.einsum('bhtd,bhdv->bhtv', qi_c, s)
        s = s * cd_c[..., :, None] + jnp.einsum('bhtd,bhtv->bhdv', ko_c, v_c)
        return s, o

    s, o = lax.scan(step, s0, (q_in, k_out, v, cd, o_intra))
    return _from_chunks(o, length), s


def _token_mixers(h, pos, s_a, s_conv, s_b, s_c, w_in, conv_a_w, a_log, dt_bias, norm_a_g,
                  w_lr2, b_lr2, norm_c_g, w_branch, w_out):
    f32 = jnp.float32
    bsz, length, _ = h.shape
    proj = jnp.einsum('bld,de->ble', h, w_in)
    (qkv_a, z_a, beta_a, dec_a, q_b, k_b, v_b, z_b,
     q_c, k_c, v_c, z_c, lr_c, mg) = jnp.split(proj, SPLIT_POINTS, axis=-1)

    qkv_a, new_conv = _causal_dwconv(qkv_a, s_conv, conv_a_w)
    qkv_a = jax.nn.silu(qkv_a.astype(f32))
    qa, ka, va = jnp.split(qkv_a, [HA * DK_A, 2 * HA * DK_A], axis=-1)
    qa = _l2norm(qa.reshape(bsz, length, HA, DK_A)) * (DK_A ** -0.5)
    ka = _l2norm(ka.reshape(bsz, length, HA, DK_A))
    va = va.reshape(bsz, length, HA, DV_A)
    beta = jax.nn.sigmoid(beta_a.astype(f32))
    ga = -jnp.exp(a_log.astype(f32)) * jax.nn.softplus(dec_a.astype(f32) + dt_bias.astype(f32))
    oa, new_a = _chunk_gated_delta(qa, ka, va, beta, ga, s_a.astype(f32), CHUNK_A)
    oa = _rmsnorm(oa, norm_a_g) * jax.nn.silu(z_a.astype(f32)).reshape(bsz, length, HA, DV_A)

    qb = _rope(q_b.astype(f32).reshape(bsz, length, HB, DK_B), pos)
    kb = _rope(k_b.astype(f32).reshape(bsz, length, HB, DK_B), pos) * (DK_B ** -0.5)
    vb = v_b.astype(f32).reshape(bsz, length, HB, DV_B)
    log_gamma = jnp.log1p(-jnp.exp2(-5.0 - jnp.arange(HB, dtype=f32)))
    gb = jnp.broadcast_to(log_gamma, (bsz, length, HB))
    ob, new_b = _chunk_scalar_decay(qb, kb, vb, gb, s_b.astype(f32), CHUNK_B)
    ob = _rmsnorm(ob) * jax.nn.silu(z_b.astype(f32)).reshape(bsz, length, HB, DV_B)

    qc = q_c.astype(f32).reshape(bsz, length, HC, DK_C) * (DK_C ** -0.5)
    kc = k_c.astype(f32).reshape(bsz, length, HC, DK_C)
    vc = v_c.astype(f32).reshape(bsz, length, HC, DV_C)
    gc = jax.nn.log_sigmoid(jnp.einsum('blr,re->ble', lr_c.astype(f32), w_lr2.astype(f32))
                            + b_lr2.astype(f32)) / GLA_TAU
    gc = gc.reshape(bsz, length, HC, DK_C)
    oc, new_c = _chunk_gla(qc, kc, vc, gc, s_c.astype(f32), CHUNK_C)
    oc = _rmsnorm(oc, norm_c_g) * jax.nn.silu(z_c.astype(f32)).reshape(bsz, length, HC, DV_C)

    branches = jnp.stack([oa.reshape(bsz, length, BRANCH_W), ob.reshape(bsz, length, BRANCH_W),
                          oc.reshape(bsz, length, BRANCH_W)], axis=2).astype(h.dtype)
    up = jnp.einsum('blnw,nwd->blnd', branches, w_branch)
    gates = jax.nn.sigmoid(mg.astype(f32)).reshape(bsz, length, N_BRANCH, D_MODEL)
    merged = jnp.sum(gates * up.astype(f32), axis=2).astype(h.dtype)
    out = jnp.einsum('bld,de->ble', merged, w_out)
    return out, new_a, new_conv, new_b, new_c


def _conv_ffn(h, s_f, w_ffn_in, ffn_conv_w, ffn_conv_b, w_ffn_out):
    u = jnp.einsum('bld,de->ble', h, w_ffn_in)
    a, b = jnp.split(u, 2, axis=-1)
    a, new_f = _causal_dwconv(a, s_f, ffn_conv_w)
    act = jax.nn.silu((a + ffn_conv_b).astype(jnp.float32)) * b.astype(jnp.float32)
    return jnp.einsum('blf,fd->bld', act.astype(h.dtype), w_ffn_out), new_f


def _run_group(x, c, pos, st_delta, st_dconv, st_ret, st_gla, st_fconv, weights, final_norm_g):
    (w_ada, b_ada, norm1_g, w_in, conv_a_w, a_log, dt_bias, norm_a_g, w_lr2, b_lr2, norm_c_g,
     w_branch, w_out, norm2_g, w_ffn_in, ffn_conv_w, ffn_conv_b, w_ffn_out) = weights
    n_delta, n_dconv, n_ret, n_gla, n_fconv = [], [], [], [], []
    for l in range(DEPTH):
        mod = (jnp.einsum('bd,de->be', jax.nn.silu(c), w_ada[l]) + b_ada[l]).astype(jnp.float32)
        sh1, sc1, g1, sh2, sc2, g2 = (m[:, None, :] for m in jnp.split(mod, 6, axis=-1))
        h = (_rmsnorm(x, norm1_g[l]) * (1.0 + sc1) + sh1).astype(x.dtype)
        mix, s_a, s_cv, s_b, s_c = _token_mixers(
            h, pos, st_delta[l], st_dconv[l], st_ret[l], st_gla[l], w_in[l], conv_a_w[l], a_log[l],
            dt_bias[l], norm_a_g[l], w_lr2[l], b_lr2[l], norm_c_g[l], w_branch[l], w_out[l])
        x = (x + g1 * mix).astype(x.dtype)
        h = (_rmsnorm(x, norm2_g[l]) * (1.0 + sc2) + sh2).astype(x.dtype)
        ffn, s_f = _conv_ffn(h, st_fconv[l], w_ffn_in[l], ffn_conv_w[l], ffn_conv_b[l], w_ffn_out[l])
        x = (x + g2 * ffn).astype(x.dtype)
        n_delta.append(s_a.astype(st_delta.dtype))
        n_dconv.append(s_cv.astype(st_dconv.dtype))
        n_ret.append(s_b.astype(st_ret.dtype))
        n_gla.append(s_c.astype(st_gla.dtype))
        n_fconv.append(s_f.astype(st_fconv.dtype))
    y = _rmsnorm(x, final_norm_g).astype(x.dtype)
    return y, jnp.stack(n_delta), jnp.stack(n_dconv), jnp.stack(n_ret), jnp.stack(n_gla), jnp.stack(n_fconv)


def setup_inputs(seed: int = 0) -> dict:
    key = jax.random.key(seed)
    ks = jax.random.split(key, 32)
    f32 = jnp.float32

    def nrm(k, shape, scale):
        return jax.random.normal(k, shape, f32) * scale

    dt = jnp.exp(jax.random.uniform(ks[14], (DEPTH, HA), f32, float(np.log(1e-3)), float(np.log(1e-1))))
    return {
        "x_prompt": nrm(ks[0], (BATCH, SEQ, D_MODEL), 1.0),
        "x_sample": nrm(ks[1], (DEC_BATCH, DEC_SEQ, D_MODEL), 1.0),
        "state_delta": nrm(ks[2], (DEPTH, DEC_BATCH, HA, DK_A, DV_A), 0.3),
        "state_delta_conv": nrm(ks[3], (DEPTH, DEC_BATCH, CONV_A - 1, CONV_CH_A), 1.0),
        "state_ret": nrm(ks[4], (DEPTH, DEC_BATCH, HB, DK_B, DV_B), 1.0),
        "state_gla": nrm(ks[5], (DEPTH, DEC_BATCH, HC, DK_C, DV_C), 0.5),
        "state_ffn_conv": nrm(ks[6], (DEPTH, DEC_BATCH, FFN_CONV - 1, D_FF), 1.0),
        "c_prompt": nrm(ks[7], (BATCH, D_MODEL), 1.0),
        "c_sample": nrm(ks[8], (DEC_BATCH, D_MODEL), 1.0),
        "w_ada": nrm(ks[9], (DEPTH, D_MODEL, 6 * D_MODEL), 0.5 * D_MODEL ** -0.5),
        "b_ada": nrm(ks[10], (DEPTH, 6 * D_MODEL), 0.02),
        "norm1_g": 1.0 + nrm(ks[11], (DEPTH, D_MODEL), 0.02),
        "w_in": nrm(ks[12], (DEPTH, D_MODEL, D_IN), D_MODEL ** -0.5),
        "conv_a_w": nrm(ks[13], (DEPTH, CONV_A, CONV_CH_A), CONV_A ** -0.5),
        "a_log": jnp.log(jax.random.uniform(ks[15], (DEPTH, HA), f32, 1.0, 16.0)),
        "dt_bias": dt + jnp.log(-jnp.expm1(-dt)),
        "norm_a_g": 1.0 + nrm(ks[16], (DEPTH, DV_A), 0.02),
        "w_lr2": nrm(ks[17], (DEPTH, GLA_RANK, HC * DK_C), GLA_RANK ** -0.5),
        "b_lr2": nrm(ks[18], (DEPTH, HC * DK_C), 0.1),
        "norm_c_g": 1.0 + nrm(ks[19], (DEPTH, DV_C), 0.02),
        "w_branch": nrm(ks[20], (DEPTH, N_BRANCH, BRANCH_W, D_MODEL), BRANCH_W ** -0.5),
        "w_out": nrm(ks[21], (DEPTH, D_MODEL, D_MODEL), D_MODEL ** -0.5),
        "norm2_g": 1.0 + nrm(ks[22], (DEPTH, D_MODEL), 0.02),
        "w_ffn_in": nrm(ks[23], (DEPTH, D_MODEL, 2 * D_FF), D_MODEL ** -0.5),
        "ffn_conv_w": nrm(ks[24], (DEPTH, FFN_CONV, D_FF), FFN_CONV ** -0.5),
        "ffn_conv_b": nrm(ks[25], (DEPTH, D_FF), 0.02),
        "w_ffn_out": nrm(ks[26], (DEPTH, D_FF, D_MODEL), D_FF ** -0.5),
        "final_norm_g": 1.0 + nrm(ks[27], (D_MODEL,), 0.02),
    }


def reference(x_prompt, x_sample, state_delta, state_delta_conv, state_ret, state_gla, state_ffn_conv,
              c_prompt, c_sample, w_ada, b_ada, norm1_g, w_in, conv_a_w, a_log, dt_bias, norm_a_g,
              w_lr2, b_lr2, norm_c_g, w_branch, w_out, norm2_g, w_ffn_in, ffn_conv_w, ffn_conv_b,
              w_ffn_out, final_norm_g):
    weights = (w_ada, b_ada, norm1_g, w_in, conv_a_w, a_log, dt_bias, norm_a_g, w_lr2, b_lr2, norm_c_g,
               w_branch, w_out, norm2_g, w_ffn_in, ffn_conv_w, ffn_conv_b, w_ffn_out)
    pos_p = jnp.arange(x_prompt.shape[1], dtype=jnp.float32)
    pos_s = PAST_LEN + jnp.arange(x_sample.shape[1], dtype=jnp.float32)
    bp = x_prompt.shape[0]
    dtp = x_prompt.dtype
    y_prompt, p_delta, p_dconv, p_ret, p_gla, p_fconv = _run_group(
        x_prompt, c_prompt, pos_p,
        jnp.zeros((DEPTH, bp, HA, DK_A, DV_A), dtp),
        jnp.zeros((DEPTH, bp, CONV_A - 1, CONV_CH_A), dtp),
        jnp.zeros((DEPTH, bp, HB, DK_B, DV_B), dtp),
        jnp.zeros((DEPTH, bp, HC, DK_C, DV_C), dtp),
        jnp.zeros((DEPTH, bp, FFN_CONV - 1, D_FF), dtp),
        weights, final_norm_g)
    y_sample, s_delta, s_dconv, s_ret, s_gla, s_fconv = _run_group(
        x_sample, c_sample, pos_s, state_delta, state_delta_conv, state_ret, state_gla, state_ffn_conv,
        weights, final_norm_g)
    return (y_prompt, y_sample, p_delta, p_dconv, p_ret, p_gla, p_fconv,
            s_delta, s_dconv, s_ret, s_gla, s_fconv)
```

```python
import numpy as np
from contextlib import ExitStack
import concourse.bass as bass
import concourse.mybir as mybir
from concourse.bass_utils import run_bass_kernel_spmd

F32 = mybir.dt.float32
BF16 = mybir.dt.bfloat16
AF = mybir.ActivationFunctionType
ALU = mybir.AluOpType
AX = mybir.AxisListType

D = 2048
SEQ = 2048
DEPTH = 4
NS = 16
NTOK = SEQ + NS
KC = D // 128
HA, HB, HC = 8, 4, 4
DFF = 5632
D_IN = 16416
EPS = 1e-6
O_QKVA, O_ZA, O_BETA, O_DEC = 0, 3072, 4096, 4104
O_QB, O_KB, O_VB, O_ZB = 4112, 4624, 5136, 6160
O_QC, O_KC, O_VC, O_ZC, O_LR, O_MG = 7184, 7696, 8208, 9232, 10256, 10272


class Ev:
    __slots__ = ("sem", "sid", "val")

    def __init__(self, sem, sid, val):
        self.sem = sem
        self.sid = sid
        self.val = val


class Buf:
    __slots__ = ("name", "w", "r", "dsem", "dcnt")

    def __init__(self, name=""):
        self.name = name
        self.w = []
        self.r = []
        self.dsem = None
        self.dcnt = 0


def _compact(evs):
    best = {}
    for ev in evs:
        o = best.get(ev.sid)
        if o is None or ev.val > o.val:
            best[ev.sid] = ev
    return list(best.values())


class KB:
    def __init__(self, nc, same_engine_sync=True):
        self.nc = nc
        self.engs = {"pe": nc.tensor, "act": nc.scalar, "dve": nc.vector, "pool": nc.gpsimd, "sp": nc.sync}
        self.sems = {}
        self.cnt = {}
        self.seen = {k: {} for k in self.engs}
        self.nsem = 0
        self.dma_bufs = []
        for k in self.engs:
            self.sems[k] = self._sem("e_" + k)
            self.cnt[k] = 0
        self.same = same_engine_sync
        self.n_ins = 0
        self.n_wait = 0
        self.free_dsems = []

    def _sem(self, name):
        self.nsem += 1
        s = self.nc.alloc_semaphore(name + "_%d" % self.nsem)
        return (s, self.nsem)

    def _wait(self, eng, ev):
        seen = self.seen[eng]
        if seen.get(ev.sid, 0) >= ev.val:
            return
        self.engs[eng].wait_ge(ev.sem, ev.val)
        seen[ev.sid] = ev.val
        self.n_wait += 1

    def _deps(self, eng, reads, writes, extra, pw=()):
        evs = []
        for b in reads:
            evs.extend(b.w)
        for b in writes:
            evs.extend(b.w)
            evs.extend(b.r)
        for b in pw:
            evs.extend(b.r)
        if extra:
            evs.extend(extra)
        own = self.sems[eng][1]
        for ev in evs:
            if ev.sid == own and (eng == "pe" or not self.same):
                continue
            self._wait(eng, ev)

    def _commit(self, ev, reads, writes, pw=()):
        for b in pw:
            b.w.append(ev)
        for b in reads:
            b.r.append(ev)
            if len(b.r) > 16:
                b.r = _compact(b.r)
        for b in pw:
            if len(b.w) > 16:
                b.w = _compact(b.w)
        for b in writes:
            b.w = [ev]
            b.r = []

    def op(self, eng, fn, reads=(), writes=(), extra=None):
        self._deps(eng, reads, writes, extra)
        ins = fn(self.engs[eng])
        sem, sid = self.sems[eng]
        ins.then_inc(sem, 1)
        self.cnt[eng] += 1
        ev = Ev(sem, sid, self.cnt[eng])
        self._commit(ev, reads, writes)
        self.n_ins += 1
        return ev

    def ops(self, eng, fns, reads=(), writes=(), extra=None):
        self._deps(eng, reads, writes, extra)
        e = self.engs[eng]
        ins = None
        for fn in fns:
            ins = fn(e)
            self.n_ins += 1
        sem, sid = self.sems[eng]
        ins.then_inc(sem, 1)
        self.cnt[eng] += 1
        ev = Ev(sem, sid, self.cnt[eng])
        self._commit(ev, reads, writes)
        return ev

    def dma(self, q, out, in_, tag, reads=(), writes=(), pw=(), extra=None, **kw):
        self._deps(q, reads, writes, extra, pw)
        t = tag
        if t.dsem is None:
            if self.free_dsems:
                t.dsem = self.free_dsems.pop()
            else:
                s = self._sem("d")
                t.dsem = [s[0], s[1], 0]
            t.dcnt = t.dsem[2]
            self.dma_bufs.append(t)
        ins = self.engs[q].dma_start(out=out, in_=in_, **kw)
        ins.then_inc(t.dsem[0], 16)
        t.dcnt += 16
        t.dsem[2] = t.dcnt
        ev = Ev(t.dsem[0], t.dsem[1], t.dcnt)
        self._commit(ev, reads, writes, pw)
        self.n_ins += 1
        return ev

    def barrier(self, release_all=True):
        for b in self.dma_bufs:
            self._wait("sp", Ev(b.dsem[0], b.dsem[1], b.dcnt))
        for o in ("pe", "act", "dve", "pool"):
            if self.cnt[o]:
                self._wait("sp", Ev(self.sems[o][0], self.sems[o][1], self.cnt[o]))
        ins = self.engs["sp"].nop()
        sem, sid = self.sems["sp"]
        ins.then_inc(sem, 1)
        self.cnt["sp"] += 1
        ev = Ev(sem, sid, self.cnt["sp"])
        for o in ("pe", "act", "dve", "pool"):
            self._wait(o, ev)
        for e in self.engs:
            seen = self.seen[e]
            for o in self.engs:
                seen[self.sems[o][1]] = self.cnt[o]
            for b in self.dma_bufs:
                seen[b.dsem[1]] = b.dcnt
            for d in self.free_dsems:
                seen[d[1]] = d[2]
        if release_all:
            for b in self.dma_bufs:
                self.free_dsems.append(b.dsem)
                b.dsem = None
            self.dma_bufs = []
        return ev


class V:
    __slots__ = ("ap", "b")

    def __init__(self, ap, b):
        self.ap = ap
        self.b = b

    def __getitem__(self, k):
        return V(self.ap[k], self.b)

    def r(self, pat, **kw):
        return V(self.ap.rearrange(pat, **kw), self.b)

    def bc(self, shape):
        return V(self.ap.to_broadcast(list(shape)), self.b)

    def us(self, ax):
        return V(self.ap.unsqueeze(ax), self.b)

    def cast(self, dt):
        return V(self.ap.bitcast(dt), self.b)


class T:
    __slots__ = ("t", "b")

    def __init__(self, t, name):
        self.t = t
        self.b = Buf(name)

    def __getitem__(self, k):
        return V(self.t[k], self.b)


C_ID, C_U, C_SL, C_ONE, C_MS, C_I16, C_DT, C_GIN, C_GOUT = 0, 128, 256, 384, 512, 640, 896, 1408, 1920
C_BD16, C_ML32, C_ML64, C_ML128, C_END = 1928, 2056, 2184, 2312, 2440


def _consts():
    c = np.zeros((128, C_END), np.float32)
    p = np.arange(128)
    c[:, C_ID:C_ID + 128] = np.eye(128)
    c[:, C_U:C_U + 128] = (p[:, None] <= p[None, :])
    c[:, C_SL:C_SL + 128] = (p[:, None] > p[None, :])
    c[:, C_ONE:C_ONE + 128] = 1.0
    c[:, C_MS:C_MS + 128] = (p[:, None] < p[None, :])
    c[:, C_I16:C_I16 + 256] = np.eye(16).reshape(1, 256)
    bd = lambda n: ((p[:, None] // n) == (p[None, :] // n)).astype(np.float32)
    c[:, C_BD16:C_BD16 + 128] = bd(16)
    c[:, C_ML32:C_ML32 + 128] = bd(32) - bd(16)
    c[:, C_ML64:C_ML64 + 128] = bd(64) - bd(32)
    c[:, C_ML128:C_ML128 + 128] = 1.0 - bd(64)
    gam = 1.0 - np.exp2(-5.0 - np.arange(HB, dtype=np.float64))
    for h in range(HB):
        dt = np.where(p[:, None] <= p[None, :], gam[h] ** np.maximum(p[None, :] - p[:, None], 0), 0.0)
        c[:, C_DT + h * 128:C_DT + (h + 1) * 128] = dt
        c[:, C_GIN + h * 128:C_GIN + (h + 1) * 128] = (gam[h] ** (p + 1.0))[None, :]
        c[:, C_GOUT + h] = gam[h] ** (127.0 - p)
    return c, gam


def _rope_tables():
    half = 64
    inv = (np.float32(10000.0) ** (-np.arange(half, dtype=np.float32) / np.float32(half))).astype(np.float32)
    pos_p = np.arange(SEQ, dtype=np.float32)
    ang = (pos_p[:, None] * inv[None, :]).astype(np.float32)
    rp = np.concatenate([np.cos(ang), np.sin(ang)], axis=1).astype(np.float32)
    angs = (np.float32(16384.0) * inv).astype(np.float32)
    rs = np.concatenate([np.cos(angs), np.sin(angs)])[None, :].repeat(NS, 0).astype(np.float32)
    return rp, rs


class Prog:
    def __init__(self, depth=DEPTH, debug=()):
        self.depth = depth
        self.debug = set(debug)
        nc = bass.Bass("TRN2", target_bir_lowering=False)
        self.nc = nc
        self.kb = KB(nc)
        self.inputs = {}
        self.outputs = {}
        self.bank_rr = 0
        self.uid = 0
        self.stop = None
        self.bank_pool = list(range(8))
        self.gam = _consts()[1]
        self._declare()

    def din(self, name, shape):
        t = self.nc.dram_tensor(name, list(shape), F32, kind="ExternalInput").ap()
        self.inputs[name] = t
        return t

    def dout(self, name, shape, dt=F32):
        t = self.nc.dram_tensor(name, list(shape), dt, kind="ExternalOutput").ap()
        self.outputs[name] = t
        return t

    def dscr(self, name, shape, dt=F32):
        kind = "ExternalOutput" if name in self.debug else "Internal"
        t = self.nc.dram_tensor(name, list(shape), dt, kind=kind).ap()
        if name in self.debug:
            self.outputs[name] = t
        return t

    def _declare(self):
        L = self.depth
        self.xp = self.din("xp", [SEQ, D])
        self.xs = self.din("xs", [NS, D])
        self.c17 = self.din("c17", [NS + 1, D])
        self.st_delta = self.din("st_delta", [L, NS, HA, 128, 128])
        self.st_dconv = self.din("st_dconv", [L, NS, 3, 3072])
        self.st_ret = self.din("st_ret", [L, NS, HB, 128, 256])
        self.st_gla = self.din("st_gla", [L, NS, HC, 128, 256])
        self.st_fconv = self.din("st_fconv", [L, NS, 2, DFF])
        self.w_ada = self.din("w_ada", [L, D, 6 * D])
        self.b_ada = self.din("b_ada", [L, 6 * D])
        self.norm1_g = self.din("norm1_g", [L, D])
        self.w_in = self.din("w_in", [L, D, D_IN])
        self.conv_a_w = self.din("conv_a_w", [L, 4, 3072])
        self.a_log = self.din("a_log", [L, HA])
        self.dt_bias = self.din("dt_bias", [L, HA])
        self.norm_a_g = self.din("norm_a_g", [L, 128])
        self.w_lr2 = self.din("w_lr2", [L, 16, 512])
        self.b_lr2 = self.din("b_lr2", [L, 512])
        self.norm_c_g = self.din("norm_c_g", [L, 256])
        self.w_branch = self.din("w_branch", [L, 3072, D])
        self.w_out = self.din("w_out", [L, D, D])
        self.norm2_g = self.din("norm2_g", [L, D])
        self.w_ffn_in = self.din("w_ffn_in", [L, D, 2 * DFF])
        self.ffn_conv_w = self.din("ffn_conv_w", [L, 3, DFF])
        self.ffn_conv_b = self.din("ffn_conv_b", [L, DFF])
        self.w_ffn_out = self.din("w_ffn_out", [L, DFF, D])
        self.final_g = self.din("final_g", [1, D])
        self.cst = self.din("cst", [128, C_END])
        self.rope_p = self.din("rope_p", [SEQ, 128])
        self.rope_s = self.din("rope_s", [NS, 128])
        self.y_p = self.dout("y_p", [SEQ, D])
        self.y_s = self.dout("y_s", [NS, D])
        self.o_delta_p = self.dout("o_delta_p", [L, HA, 128, 128])
        self.o_dconv_p = self.dout("o_dconv_p", [L, 3, 3072])
        self.o_ret_p = self.dout("o_ret_p", [L, HB, 128, 256])
        self.o_gla_p = self.dout("o_gla_p", [L, HC, 128, 256])
        self.o_fconv_p = self.dout("o_fconv_p", [L, 2, DFF])
        self.o_delta_s = self.dout("o_delta_s", [L, NS, HA, 128, 128])
        self.o_dconv_s = self.dout("o_dconv_s", [L, NS, 3, 3072])
        self.o_ret_s = self.dout("o_ret_s", [L, NS, HB, 128, 256])
        self.o_gla_s = self.dout("o_gla_s", [L, NS, HC, 128, 256])
        self.o_fconv_s = self.dout("o_fconv_s", [L, NS, 2, DFF])
        self.s_x = self.dscr("s_x", [NTOK, D])
        self.s_mod = self.dscr("s_mod", [L, NS + 1, 6 * D])
        self.s_qkva = self.dscr("s_qkva", [3 + SEQ, 3072])
        self.s_qkva_s = self.dscr("s_qkva_s", [NS, 3072])
        self.s_proj = self.dscr("s_proj", [NTOK, D_IN])
        self.s_brT = self.dscr("s_brT", [17, 128, 24 * 128], BF16)
        self.s_ua = self.dscr("s_ua", [2 + SEQ, DFF])
        self.s_ua_s = self.dscr("s_ua_s", [NS, DFF])
        self.s_ub = self.dscr("s_ub", [NTOK, DFF])
        self.s_actT = self.dscr("s_actT", [17, 128, 44 * 128], BF16)

    def sb(self, es, name, shape, dt=F32):
        self.uid += 1
        name = "%s_%d" % (name, self.uid)
        t = es.enter_context(self.nc.sbuf_tensor(name, list(shape), dt))
        return T(t, name)

    @staticmethod
    def _a(x):
        return x.ap if isinstance(x, V) else x

    @staticmethod
    def _bufs(*xs):
        out = []
        for x in xs:
            if isinstance(x, V) and x.b not in out:
                out.append(x.b)
        return out

    def tt(self, eng, out, a, b, op):
        return self.kb.op(eng, lambda e: e.tensor_tensor(out=out.ap, in0=a.ap, in1=b.ap, op=op),
                          reads=self._bufs(a, b), writes=[out.b])

    def ts(self, eng, out, a, s1, op0, s2=None, op1=None, accum=None):
        kw = {}
        if op1 is not None:
            kw["op1"] = op1
        if accum is not None:
            kw["accum_out"] = accum.ap
        return self.kb.op(eng, lambda e: e.tensor_scalar(out=out.ap, in0=a.ap, scalar1=self._a(s1), scalar2=self._a(s2),
                                                         op0=op0, **kw),
                          reads=self._bufs(a, s1, s2), writes=self._bufs(out, accum))

    def stt(self, eng, out, a, s, b, op0, op1):
        return self.kb.op(eng, lambda e: e.scalar_tensor_tensor(out=out.ap, in0=a.ap, scalar=self._a(s), in1=b.ap,
                                                                op0=op0, op1=op1),
                          reads=self._bufs(a, s, b), writes=[out.b])

    def act(self, out, a, func, bias=None, scale=None, accum=None):
        kw = {}
        if bias is not None:
            kw["bias"] = self._a(bias)
        if scale is not None:
            kw["scale"] = self._a(scale)
        if accum is not None:
            kw["accum_out"] = accum.ap
        return self.kb.op("act", lambda e: e.activation(out=out.ap, in_=a.ap, func=func, **kw),
                          reads=self._bufs(a, bias, scale), writes=self._bufs(out, accum))

    def cp(self, eng, out, a):
        if eng == "act":
            return self.act(out, a, AF.Copy)
        return self.kb.op(eng, lambda e: e.tensor_copy(out=out.ap, in_=a.ap), reads=[a.b], writes=[out.b])

    def memset(self, eng, out, val):
        return self.kb.op(eng, lambda e: e.memset(out.ap, val), writes=[out.b])

    def recip(self, out, a):
        return self.kb.op("dve", lambda e: e.reciprocal(out=out.ap, in_=a.ap), reads=[a.b], writes=[out.b])

    def mm(self, out_bank, items):
        fns = []
        rd = []
        for (o, l, r, st, sp) in items:
            fns.append(lambda e, o=o, l=l, r=r, st=st, sp=sp: e.matmul(o.ap, lhsT=l.ap, rhs=r.ap, start=st, stop=sp))
            for x in (l, r):
                if x.b not in rd:
                    rd.append(x.b)
        return self.kb.ops("pe", fns, reads=rd, writes=[out_bank.b])

    def tr(self, out_bank, items, ident):
        fns = []
        rd = [ident.b]
        for (o, a) in items:
            fns.append(lambda e, o=o, a=a: e.transpose(out=o.ap, in_=a.ap, identity=ident.ap))
            if a.b not in rd:
                rd.append(a.b)
        return self.kb.ops("pe", fns, reads=rd, writes=[out_bank.b])

    def load(self, dst, src_ap, dram=(), q="sp", partial=False, **kw):
        if partial:
            return self.kb.dma(q, dst.ap, src_ap, tag=dst.b, reads=list(dram), pw=[dst.b], **kw)
        return self.kb.dma(q, dst.ap, src_ap, tag=dst.b, reads=list(dram), writes=[dst.b], **kw)

    def store(self, dst_ap, src, dram=(), q="sp", **kw):
        return self.kb.dma(q, dst_ap, src.ap, tag=src.b, reads=[src.b], pw=list(dram), **kw)

    def d2d(self, dst_ap, src_ap, tag, rd=(), wr=()):
        return self.kb.dma("sp", dst_ap, src_ap, tag=tag, reads=list(rd), pw=list(wr))

    def bank(self):
        pool = self.bank_pool
        b = self.ps[pool[self.bank_rr % len(pool)]]
        self.bank_rr += 1
        return b

    TILES = [(i, i * 128, 128) for i in range(16)] + [(16, SEQ, NS)]

    def xrows(self, l, i, c0, tp):
        if l == 0:
            return (self.xp[c0:c0 + tp, :] if i < 16 else self.xs[:, :]), None
        return self.s_x[c0:c0 + tp, :], self.bx[i]

    def build(self):
        nc, kb = self.nc, self.kb
        self.ps = [T(nc.alloc_psum_tensor("ps%d" % i, [128, 512], F32), "ps%d" % i) for i in range(8)]
        self.bx = [Buf("x%d" % i) for i in range(17)]
        self.bmod = [Buf("mod%d" % l) for l in range(self.depth)]
        self.bqkva = [Buf("qkva%d" % i) for i in range(17)]
        self.bproj = [Buf("proj%d" % i) for i in range(17)]
        self.bbrT = [Buf("brT%d" % i) for i in range(17)]
        self.bua = [Buf("ua%d" % i) for i in range(17)]
        self.bub = [Buf("ub%d" % i) for i in range(17)]
        self.bactT = [Buf("actT%d" % i) for i in range(17)]
        self.bout = Buf("outs")
        self.out_evs = []
        with ExitStack() as g:
            self.cst_t = self.sb(g, "cst_sb", [128, C_END])
            self.idb = self.sb(g, "idb", [128, 128], BF16)
            self.cT = self.sb(g, "cT", [128, KC, NS + 1], BF16)
            self.load(self.cst_t[:], self.cst)
            self.cp("dve", self.idb[:], self.cst_t[:, C_ID:C_ID + 128])
            self.idf = self.cst_t[:, C_ID:C_ID + 128]
            self.prologue()
            if self.stop not in ("pro0", "pro"):
                for l in range(self.depth):
                    self.layer(l)
                self.final_norm()
            kb.barrier()
        return nc

    def prologue(self):
        kb = self.kb
        with ExitStack() as es:
            z = self.sb(es, "zrow", [3, DFF])
            self.memset("dve", z[:], 0.0)
            self.store(self.s_qkva[0:3, :], z[:, 0:3072], dram=[self.bqkva[0]])
            self.store(self.s_ua[0:2, :], z[0:2, :], dram=[self.bua[0]])
            c = self.sb(es, "c17_sb", [NS + 1, D])
            cb = self.sb(es, "c17b", [NS + 1, D], BF16)
            self.load(c[:], self.c17)
            self.act(cb[:], c[:], AF.Silu)
            for half in range(2):
                bk = self.bank()
                pv = bk[:].cast(BF16)
                self.tr(bk, [(pv[:, k * 32:k * 32 + 17], cb[:, (half * 8 + k) * 128:(half * 8 + k + 1) * 128]) for k in range(8)],
                        self.idb[0:NS + 1, 0:NS + 1])
                self.cp("act", self.cT[:, half * 8:half * 8 + 8, :], pv[:, 0:256].r("p (k c) -> p k c", c=32)[:, :, 0:17])
            if self.stop == "pro0":
                self.store(self.s_mod[0, :, 0:16 * 17].rearrange("a (k c) -> a k c", c=17)[0:1].rearrange("a k c -> (a k) c"), self.cT[0:16, 0, :].r("p c -> p c"), dram=[self.bmod[0]]) if False else None
                kb.barrier()
                return
            wsl = [self.sb(es, "wada%d" % i, [128, KC, 512], BF16) for i in range(2)]
            bsl = [self.sb(es, "bada%d" % i, [NS + 1, 512]) for i in range(2)]
            osl = [self.sb(es, "moda%d" % i, [NS + 1, 512]) for i in range(2)]
            n = 0
            for l in range(self.depth):
                for j in range(24):
                    w = wsl[n % 2]
                    self.load(w[:], self.w_ada[l, :, j * 512:(j + 1) * 512].rearrange("(kc p) n -> p kc n", p=128), q="pool")
                    bb = bsl[n % 2]
                    self.load(bb[:], self.b_ada[l:l + 1, j * 512:(j + 1) * 512].partition_broadcast(NS + 1))
                    bk = self.bank()
                    self.mm(bk, [(bk[0:NS + 1, :], self.cT[:, k, :], w[:, k, :], k == 0, k == KC - 1) for k in range(KC)])
                    o = osl[n % 2]
                    self.tt("dve", o[:], bk[0:NS + 1, :], bb[:], ALU.add)
                    self.store(self.s_mod[l, :, j * 512:(j + 1) * 512], o[:], dram=[self.bmod[l]])
                    n += 1
            kb.barrier()

    def xsrc(self, l, which, i, c0, tp):
        if which == 1 and l == 0:
            return (self.xp[c0:c0 + tp, :] if i < 16 else self.xs[:, :]), []
        return self.s_x[c0:c0 + tp, :], [self.bx[i]]

    def norm_phase(self, l, which, XT):
        kb = self.kb
        gam = self.norm1_g if which == 1 else self.norm2_g
        o_sh, o_sc = (0, D) if which == 1 else (3 * D, 4 * D)
        with ExitStack() as es:
            gsP = self.sb(es, "gsP", [128, D])
            shP = self.sb(es, "shP", [128, D])
            gsS = self.sb(es, "gsS", [NS, D])
            shS = self.sb(es, "shS", [NS, D])
            gt = self.sb(es, "gt", [128, D])
            xsl = [self.sb(es, "nx%d" % i, [128, D]) for i in range(2)]
            hsl = [self.sb(es, "nh%d" % i, [128, D], BF16) for i in range(2)]
            ssq = [self.sb(es, "nssq%d" % i, [128, 2]) for i in range(2)]
            self.load(gt[:], gam[l:l + 1, :].partition_broadcast(128))
            self.load(gsP[:], self.s_mod[l, NS:NS + 1, o_sc:o_sc + D].partition_broadcast(128), dram=[self.bmod[l]])
            self.load(shP[:], self.s_mod[l, NS:NS + 1, o_sh:o_sh + D].partition_broadcast(128), dram=[self.bmod[l]])
            self.load(gsS[:], self.s_mod[l, 0:NS, o_sc:o_sc + D], dram=[self.bmod[l]])
            self.load(shS[:], self.s_mod[l, 0:NS, o_sh:o_sh + D], dram=[self.bmod[l]])
            self.stt("dve", gsP[:], gsP[:], 1.0, gt[:], ALU.add, ALU.mult)
            self.stt("dve", gsS[:], gsS[:], 1.0, gt[0:NS, :], ALU.add, ALU.mult)
            for (i, c0, tp) in self.TILES:
                xt, hb, sq = xsl[i % 2], hsl[i % 2], ssq[i % 2]
                gs, sh = (gsP, shP) if i < 16 else (gsS, shS)
                src, dr = self.xsrc(l, which, i, c0, tp)
                self.load(xt[0:tp, :], src, dram=dr)
                self.memset("dve", sq[0:tp, :], 0.0)
                self.act(hb[0:tp, :], xt[0:tp, :], AF.Square, accum=sq[0:tp, 0:1])
                self.act(sq[0:tp, 1:2], sq[0:tp, 0:1], AF.Sqrt, bias=EPS, scale=1.0 / D)
                self.recip(sq[0:tp, 1:2], sq[0:tp, 1:2])
                self.stt("dve", xt[0:tp, :], xt[0:tp, :], sq[0:tp, 1:2], gs[0:tp, :], ALU.mult, ALU.mult)
                self.tt("dve", hb[0:tp, :], xt[0:tp, :], sh[0:tp, :], ALU.add)
                for half in range(2):
                    bk = self.bank()
                    pv = bk[:].cast(BF16)
                    self.tr(bk, [(pv[:, k * 128:k * 128 + tp], hb[0:tp, (half * 8 + k) * 128:(half * 8 + k + 1) * 128])
                                 for k in range(8)], self.idb[0:tp, 0:tp])
                    self.cp("act" if half == 0 else "dve", XT[:, half * 8:half * 8 + 8, c0:c0 + tp],
                            pv.r("p (k c) -> p k c", c=128)[:, :, 0:tp])
            kb.barrier()

    G1_SEGS = [(0, 3072, "copy"), (3072, 1024, "silu"), (4096, 16, "copy"), (4112, 1024, "copy"), (5136, 1024, "copy"),
               (6160, 1024, "silu"), (7184, 1024, "copy"), (8208, 1024, "copy"), (9232, 1024, "silu"),
               (10256, 16, "copy"), (10272, 6144, "sigmoid")]

    def gemm_res(self, es, XT, kc, blocks, wsrc, evac, tiles=None, nw=2, wn=512):
        wsl = [self.sb(es, "gw%d" % i, [128, kc, wn], BF16) for i in range(nw)]
        tiles = self.TILES if tiles is None else tiles
        for jb, blk in enumerate(blocks):
            w = wsl[jb % nw]
            n = blk[1]
            self.load(w[:, :, 0:n], wsrc(blk).rearrange("(kc p) n -> p kc n", p=128), q="pool")
            for (i, c0, tp) in tiles:
                bk = self.bank()
                self.mm(bk, [(bk[0:tp, 0:n], XT[:, k, c0:c0 + tp], w[:, k, 0:n], k == 0, k == kc - 1) for k in range(kc)])
                evac((i, c0, tp), blk, bk)

    def g1(self, l, XT):
        kb = self.kb
        blocks = []
        for (c, n, f) in self.G1_SEGS:
            for o in range(0, n, 512):
                blocks.append((c + o, min(512, n - o), f))
        import os
        if os.environ.get("G1_BLK"):
            a, b = os.environ["G1_BLK"].split(":")
            blocks = blocks[int(a):int(b)]
        with ExitStack() as es:
            stg = [self.sb(es, "g1s%d" % i, [128, 512]) for i in range(4)]
            cnt = [0]

            def evac(tile, blk, bk):
                i, c0, tp = tile
                c, n, f = blk
                s = stg[cnt[0] % 4]
                if f == "silu":
                    self.act(s[0:tp, 0:n], bk[0:tp, 0:n], AF.Silu)
                elif f == "sigmoid":
                    self.act(s[0:tp, 0:n], bk[0:tp, 0:n], AF.Sigmoid)
                else:
                    self.cp("dve" if cnt[0] % 2 == 0 else "act", s[0:tp, 0:n], bk[0:tp, 0:n])
                cnt[0] += 1
                if c < 3072:
                    if i < 16:
                        self.store(self.s_qkva[3 + c0:3 + c0 + tp, c:c + n], s[0:tp, 0:n], dram=[self.bqkva[i]])
                    else:
                        self.store(self.s_qkva_s[:, c:c + n], s[0:tp, 0:n], dram=[self.bqkva[i]])
                else:
                    self.store(self.s_proj[c0:c0 + tp, c:c + n], s[0:tp, 0:n], dram=[self.bproj[i]])

            self.gemm_res(es, XT, KC, blocks, lambda blk: self.w_in[l, :, blk[0]:blk[0] + blk[1]], evac)
            cs1 = self.sb(es, "cst1", [3, 3072])
            cs2 = self.sb(es, "cst2", [NS, 3, 3072])
            self.load(cs1[:], self.s_qkva[SEQ:SEQ + 3, :], dram=[self.bqkva[15]])
            self.out_store(self.o_dconv_p[l], cs1[:])
            self.load(cs2[:, 0:2, :], self.st_dconv[l, :, 1:3, :])
            self.load(cs2[:, 2, :], self.s_qkva_s[:, :], dram=[self.bqkva[16]], partial=True)
            self.out_store(self.o_dconv_s[l], cs2[:])
            kb.barrier()

    def out_store(self, dst_ap, src, q="sp"):
        return self.kb.dma(q, dst_ap, src.ap, tag=src.b, reads=[src.b])

    def gated_norm_store(self, i, tp, o, z_src_col, gain, nh, dv, slot, tmp):
        sc, junk, z, br, brs = tmp
        c0 = self.TILES[i][1]
        self.load(z[0:tp, :], self.s_proj[c0:c0 + tp, z_src_col:z_src_col + 1024], dram=[self.bproj[i]])
        self.memset("dve", sc[0:tp, 0:nh], 0.0)
        for h in range(nh):
            self.act(junk[0:tp, 0:dv], o[0:tp, h * dv:(h + 1) * dv], AF.Square, accum=sc[0:tp, h:h + 1])
        self.act(sc[0:tp, 8:8 + nh], sc[0:tp, 0:nh], AF.Sqrt, bias=EPS, scale=1.0 / dv)
        self.recip(sc[0:tp, 8:8 + nh], sc[0:tp, 8:8 + nh])
        o3 = o[0:tp, :].r("p (h d) -> p h d", d=dv)
        self.tt("dve", o3, o3, sc[0:tp, 8:8 + nh].us(2).bc([tp, nh, dv]), ALU.mult)
        if gain is not None:
            self.tt("pool", o3, o3, gain[0:tp, :].us(1).bc([tp, nh, dv]), ALU.mult)
        self.tt("dve", br[0:tp, :], o[0:tp, :], z[0:tp, :], ALU.mult)
        bk = self.bank()
        pv = bk[:].cast(BF16)
        self.tr(bk, [(pv[:, k * 128:k * 128 + tp], br[0:tp, k * 128:(k + 1) * 128]) for k in range(8)], self.idb[0:tp, 0:tp])
        self.cp("act", brs[:, :, 0:tp], pv.r("p (k c) -> p k c", c=128)[:, :, 0:tp])
        dst = self.s_brT[i, :, slot * 1024:(slot + 1) * 1024].rearrange("p (k c) -> p k c", c=128)[:, :, 0:tp]
        self.store(dst, brs[:, :, 0:tp], dram=[self.bbrT[i]])

    def bcast_cols(self, out, x, ncol, tmpX):
        X3 = tmpX[0:NS, 0:NS * ncol].r("p (s c) -> p s c", c=ncol)
        self.tt("dve", X3, x.us(1).bc([NS, NS, ncol]), self.idf[0:NS, 0:NS].us(2).bc([NS, NS, ncol]), ALU.mult)
        bk = self.bank()
        self.mm(bk, [(bk[:, 0:NS * ncol], self.cst_t[0:NS, C_ONE:C_ONE + 128], tmpX[0:NS, 0:NS * ncol], True, True)])
        self.cp("act", out, bk[:, 0:NS * ncol])

    def mixA(self, l):
        kb = self.kb
        C = self.cst_t
        U, SL, ONE, MS, IDF = (C[:, C_U:C_U + 128], C[:, C_SL:C_SL + 128], C[:, C_ONE:C_ONE + 128],
                               C[:, C_MS:C_MS + 128], self.idf)
        with ExitStack() as es:
            cw = self.sb(es, "cw", [128, 4, 3072])
            xq = [self.sb(es, "xq%d" % i, [128, 4, 1024]) for i in range(2)]
            qkv = self.sb(es, "qkv", [128, 3072])
            tmpc = self.sb(es, "tmpc", [128, 1024])
            za = self.sb(es, "za", [128, 1024])
            sm = self.sb(es, "smA", [128, 16])
            par = self.sb(es, "parA", [128, 16])
            negA = self.sb(es, "negA", [128, 8])
            gab = self.sb(es, "gab", [128, 128])
            oall = self.sb(es, "oallA", [128, 1024])
            br = self.sb(es, "brA", [128, 1024], BF16)
            brs = self.sb(es, "brsA", [128, 8, 128], BF16)
            junk = self.sb(es, "junkA", [128, 256], BF16)
            sc = self.sb(es, "scA", [128, 64])
            sc2 = self.sb(es, "sc2A", [128, 64])
            sc3 = self.sb(es, "sc3A", [128, 16])
            self.load(cw[:].r("p s c -> p (s c)"), self.conv_a_w[l:l + 1].rearrange("a s c -> a (s c)").partition_broadcast(128))
            self.load(par[:, 0:8], self.a_log[l:l + 1, :].partition_broadcast(128))
            self.load(par[:, 8:16], self.dt_bias[l:l + 1, :].partition_broadcast(128), partial=True)
            self.load(gab[:], self.norm_a_g[l:l + 1, :].partition_broadcast(128))
            self.act(negA[:], par[:, 0:8], AF.Exp)
            self.ts("dve", negA[:], negA[:], -1.0, ALU.mult)
            nconv = [0]

            def front(i, c0, tp):
                for pt in range(3):
                    pc = pt * 1024
                    x4 = xq[nconv[0] % 2]
                    nconv[0] += 1
                    for s in range(4):
                        if i < 16:
                            src = self.s_qkva[c0 + s:c0 + s + 128, pc:pc + 1024]
                            dr = [self.bqkva[max(i - 1, 0)], self.bqkva[i]]
                        elif s < 3:
                            src, dr = self.st_dconv[l, :, s, pc:pc + 1024], []
                        else:
                            src, dr = self.s_qkva_s[:, pc:pc + 1024], [self.bqkva[16]]
                        self.load(x4[0:tp, s, :], src, dram=dr, partial=(s > 0))
                    acc = qkv[0:tp, pc:pc + 1024]
                    self.tt("pool", acc, x4[0:tp, 0, :], cw[0:tp, 0, pc:pc + 1024], ALU.mult)
                    for s in range(1, 4):
                        self.tt("pool", tmpc[0:tp, :], x4[0:tp, s, :], cw[0:tp, s, pc:pc + 1024], ALU.mult)
                        self.tt("dve", acc, acc, tmpc[0:tp, :], ALU.add)
                    self.act(acc, acc, AF.Silu)
                self.memset("dve", sc2[0:tp, 0:16], 0.0)
                for h in range(16):
                    self.act(junk[0:tp, 0:128], qkv[0:tp, h * 128:(h + 1) * 128], AF.Square, accum=sc2[0:tp, h:h + 1])
                self.act(sc2[0:tp, 16:32], sc2[0:tp, 0:16], AF.Sqrt, bias=EPS)
                self.recip(sc2[0:tp, 16:32], sc2[0:tp, 16:32])
                self.ts("dve", sc2[0:tp, 16:24], sc2[0:tp, 16:24], 128.0 ** -0.5, ALU.mult)
                qk3 = qkv[0:tp, 0:2048].r("p (h d) -> p h d", d=128)
                self.tt("dve", qk3, qk3, sc2[0:tp, 16:32].us(2).bc([tp, 16, 128]), ALU.mult)
                self.load(sm[0:tp, :], self.s_proj[c0:c0 + tp, O_BETA:O_BETA + 16], dram=[self.bproj[i]])
                self.act(sc[0:tp, 0:8], sm[0:tp, 0:8], AF.Sigmoid)
                self.tt("dve", sc[0:tp, 8:16], sm[0:tp, 8:16], par[0:tp, 8:16], ALU.add)
                self.act(sc[0:tp, 8:16], sc[0:tp, 8:16], AF.Exp)
                self.act(sc[0:tp, 8:16], sc[0:tp, 8:16], AF.Ln, bias=1.0)
                self.tt("dve", sc[0:tp, 8:16], sc[0:tp, 8:16], negA[0:tp, :], ALU.mult)

            def back(i, tp):
                self.gated_norm_store(i, tp, oall, O_ZA, gab, 8, 128, 0, (sc3, junk, za, br, brs))

            with ExitStack() as e1:
                S = self.sb(e1, "SA", [128, 1024])
                EdT = self.sb(e1, "EdT", [128, 1024])
                EdTs = self.sb(e1, "EdTs", [128, 1024])
                Ug = self.sb(e1, "Ug", [128, 1024])
                names = ["kb_", "qin", "kout", "Rw", "Rv", "kT", "kbT", "qT", "qinT", "LT", "L", "M0", "M1", "MT0", "MT1",
                         "PT", "wT", "uv", "qkT", "u", "L16", "L16T", "X", "Y", "Bm"]
                t = {n: self.sb(e1, "A_" + n, [128, 512]) for n in names}
                self.memset("dve", S[:], 0.0)
                r3 = lambda v: v.r("p (h d) -> p h d", d=128)
                for (i, c0, tp) in self.TILES[:16]:
                    front(i, c0, tp)
                    g = sc[:, 8:16]
                    bk = self.bank()
                    self.mm(bk, [(bk[:, 0:8], U, g, True, True), (bk[:, 8:16], ONE, g, True, True)])
                    self.cp("act", sc[:, 16:24], bk[:, 0:8])
                    self.act(sc[:, 24:32], bk[:, 0:8], AF.Exp)
                    self.act(sc[:, 32:40], bk[:, 8:16], AF.Exp)
                    self.tt("dve", sc[:, 40:48], bk[:, 8:16], sc[:, 16:24], ALU.subtract)
                    self.act(sc[:, 40:48], sc[:, 40:48], AF.Exp)
                    self.tt("dve", sc[:, 48:56], sc[:, 0:8], sc[:, 24:32], ALU.mult)
                    self.tt("dve", r3(Ug[:]), U.us(1).bc([128, 8, 128]), g.us(2).bc([128, 8, 128]), ALU.mult)
                    for half in range(2):
                        bk = self.bank()
                        self.mm(bk, [(bk[:], SL, Ug[:, half * 512:(half + 1) * 512], True, True)])
                        self.act(EdT[:, half * 512:(half + 1) * 512], bk[:], AF.Exp)
                    self.tt("dve", r3(EdTs[:]), r3(EdT[:]), MS.us(1).bc([128, 8, 128]), ALU.mult)
                    self.tt("pool", r3(EdT[:]), r3(EdT[:]), U.us(1).bc([128, 8, 128]), ALU.mult)
                    for hg in range(2):
                        h0 = hg * 4
                        qh = qkv[:, h0 * 128:h0 * 128 + 512]
                        kh = qkv[:, 1024 + h0 * 128:1024 + h0 * 128 + 512]
                        vh = qkv[:, 2048 + h0 * 128:2048 + h0 * 128 + 512]
                        bc4 = lambda col: sc[:, col + h0:col + h0 + 4].us(2).bc([128, 4, 128])
                        self.tt("dve", r3(t["kb_"][:]), r3(kh), bc4(0), ALU.mult)
                        self.tt("pool", r3(t["qin"][:]), r3(qh), bc4(24), ALU.mult)
                        self.tt("dve", r3(t["kout"][:]), r3(kh), bc4(40), ALU.mult)
                        self.tt("pool", r3(t["Rw"][:]), r3(kh), bc4(48), ALU.mult)
                        self.tt("dve", r3(t["Rv"][:]), r3(vh), bc4(0), ALU.mult)
                        hv = lambda v, h: v[:, h * 128:(h + 1) * 128]
                        for (dst, src) in (("kT", kh), ("kbT", t["kb_"][:]), ("qT", qh), ("qinT", t["qin"][:])):
                            bk = self.bank()
                            self.tr(bk, [(hv(bk, h), hv(src, h)) for h in range(4)], IDF)
                            self.cp("act", t[dst][:], bk[:])
                        bk = self.bank()
                        self.mm(bk, [(hv(bk, h), hv(t["kT"], h), hv(t["kbT"], h), True, True) for h in range(4)])
                        self.tt("dve", t["LT"][:], bk[:], EdTs[:, h0 * 128:h0 * 128 + 512], ALU.mult)
                        bk = self.bank()
                        self.tr(bk, [(hv(bk, h), hv(t["LT"], h)) for h in range(4)], IDF)
                        self.cp("act", t["L"][:], bk[:])
                        m4 = lambda col: C[:, col:col + 128].us(1).bc([128, 4, 128])
                        self.tt("pool", r3(t["L16"][:]), r3(t["L"][:]), m4(C_BD16), ALU.mult)
                        self.tt("dve", r3(t["L16T"][:]), r3(t["LT"][:]), m4(C_BD16), ALU.mult)
                        self.stt("dve", r3(t["PT"][:]), r3(t["L16T"][:]), -1.0, IDF.us(1).bc([128, 4, 128]), ALU.mult, ALU.add)
                        M, MT = t["L16"], t["L16T"]
                        for r in range(3):
                            M2, M2T = t["M%d" % (r % 2)], t["MT%d" % (r % 2)]
                            bkM = self.bank()
                            self.mm(bkM, [(hv(bkM, h), hv(MT, h), hv(M, h), True, True) for h in range(4)])
                            if r < 2:
                                bkT = self.bank()
                                self.mm(bkT, [(hv(bkT, h), hv(M, h), hv(MT, h), True, True) for h in range(4)])
                            self.cp("act", M2[:], bkM[:])
                            if r < 2:
                                self.cp("dve", M2T[:], bkT[:])
                            bkP = self.bank()
                            self.mm(bkP, [(hv(bkP, h), hv(M2, h), hv(t["PT"], h), True, True) for h in range(4)])
                            self.tt("dve", t["PT"][:], t["PT"][:], bkP[:], ALU.add)
                            M, MT = M2, M2T
                        for lv, mcol in enumerate((C_ML32, C_ML64, C_ML128)):
                            bk = self.bank()
                            self.tr(bk, [(hv(bk, h), hv(t["PT"], h)) for h in range(4)], IDF)
                            self.cp("act", t["X"][:], bk[:])
                            self.tt("pool", r3(t["Bm"][:]), r3(t["L"][:]), m4(mcol), ALU.mult)
                            bk = self.bank()
                            self.mm(bk, [(hv(bk, h), hv(t["Bm"], h), hv(t["PT"], h), True, True) for h in range(4)])
                            self.cp("act", t["Y"][:], bk[:])
                            bk = self.bank()
                            self.mm(bk, [(hv(bk, h), hv(t["X"], h), hv(t["Y"], h), True, True) for h in range(4)])
                            self.tt("dve", t["PT"][:], t["PT"][:], bk[:], ALU.subtract)
                        bk = self.bank()
                        self.mm(bk, [(hv(bk, h), hv(t["Rw"], h), hv(t["PT"], h), True, True) for h in range(4)])
                        self.cp("act", t["wT"][:], bk[:])
                        bk = self.bank()
                        self.mm(bk, [(hv(bk, h), hv(t["PT"], h), hv(t["Rv"], h), True, True) for h in range(4)])
                        self.cp("act", t["uv"][:], bk[:])
                        bk = self.bank()
                        self.mm(bk, [(hv(bk, h), hv(t["kT"], h), hv(t["qT"], h), True, True) for h in range(4)])
                        self.tt("dve", t["qkT"][:], bk[:], EdT[:, h0 * 128:h0 * 128 + 512], ALU.mult)
                        Sh = S[:, h0 * 128:h0 * 128 + 512]
                        bk = self.bank()
                        self.mm(bk, [(hv(bk, h), hv(t["wT"], h), hv(Sh, h), True, True) for h in range(4)])
                        self.tt("dve", t["u"][:], t["uv"][:], bk[:], ALU.subtract)
                        bk = self.bank()
                        items = []
                        for h in range(4):
                            items.append((hv(bk, h), hv(t["qinT"], h), hv(Sh, h), True, False))
                            items.append((hv(bk, h), hv(t["qkT"], h), hv(t["u"], h), False, True))
                        self.mm(bk, items)
                        self.cp("act", oall[:, h0 * 128:h0 * 128 + 512], bk[:])
                        bk = self.bank()
                        self.mm(bk, [(hv(bk, h), hv(t["kout"], h), hv(t["u"], h), True, True) for h in range(4)])
                        for h in range(4):
                            self.stt("dve", hv(Sh, h), hv(Sh, h), sc[:, 32 + h0 + h:33 + h0 + h], hv(bk, h), ALU.mult, ALU.add)
                    back(i, tp)
                self.out_store(self.o_delta_p[l].rearrange("h k v -> k h v"), r3(S[:]))
                kb.barrier()
            with ExitStack() as e2:
                i, c0, tp = self.TILES[16]
                kTs = self.sb(e2, "kTs", [128, 8, NS])
                qTs = self.sb(e2, "qTs", [128, 8, NS])
                kTd = self.sb(e2, "kTd", [128, 8 * NS * NS])
                qTd = self.sb(e2, "qTd", [128, 8 * NS * NS])
                Sst = [self.sb(e2, "Sst%d" % j, [128, 2, 8, 128]) for j in range(2)]
                Snw = [self.sb(e2, "Snw%d" % j, [128, 2, 8, 128]) for j in range(2)]
                kd = self.sb(e2, "kd", [NS, 8, 2, 128])
                u = self.sb(e2, "uS", [NS, 1024])
                tX = self.sb(e2, "tX", [NS, 128])
                egbc = self.sb(e2, "egbc", [128, 128])
                front(i, c0, tp)
                self.act(sc[0:tp, 16:24], sc[0:tp, 8:16], AF.Exp)
                self.bcast_cols(egbc[:], sc[0:NS, 16:24], 8, tX)
                for (dst, dd, off) in ((kTs, kTd, 1024), (qTs, qTd, 0)):
                    bk = self.bank()
                    self.tr(bk, [(bk[:, h * NS:(h + 1) * NS], qkv[0:NS, off + h * 128:off + (h + 1) * 128]) for h in range(8)],
                            IDF[0:NS, 0:NS])
                    self.cp("act", dst[:].r("p h s -> p (h s)"), bk[:, 0:8 * NS])
                    d4 = dd[:].r("p (h a b) -> p h a b", a=NS, b=NS)
                    self.tt("dve", d4, dst[:].us(2).bc([128, 8, NS, NS]),
                            C[:, C_I16:C_I16 + 256].r("p (a b) -> p a b", b=NS).us(1).bc([128, 8, NS, NS]), ALU.mult)
                kTd4 = kTd[:].r("p (h a b) -> p h a b", a=NS, b=NS)
                qTd4 = qTd[:].r("p (h a b) -> p h a b", a=NS, b=NS)
                self.bank_pool = [0, 1, 2, 3]
                bR = [self.ps[4], self.ps[5]]
                bO = [self.ps[6], self.ps[7]]
                for cch in range(8):
                    St = Sst[cch % 2]
                    self.load(St[:], self.st_delta[l, 2 * cch:2 * cch + 2].rearrange("s h k v -> k s h v"))
                    for half in range(2):
                        items = []
                        for sl in range(2):
                            s = 2 * cch + sl
                            for h in range(half * 4, half * 4 + 4):
                                items.append((bR[half][0:NS, (h % 4) * 128:(h % 4 + 1) * 128], kTd4[:, h, s, :], St[:, sl, h, :],
                                              s == 0 and h % 4 == 0, s == NS - 1))
                        self.mm(bR[half], items)
                u3 = u[:].r("p (h d) -> p h d", d=128)
                for half in range(2):
                    uh = u[:, half * 512:(half + 1) * 512].r("p (h d) -> p h d", d=128)
                    self.tt("dve", uh, bR[half][0:NS, :].r("p (h d) -> p h d", d=128),
                            sc[0:NS, 16 + half * 4:20 + half * 4].us(2).bc([NS, 4, 128]), ALU.mult)
                self.tt("dve", u[:], qkv[0:NS, 2048:3072], u[:], ALU.subtract)
                self.tt("dve", u3, u3, sc[0:NS, 0:8].us(2).bc([NS, 8, 128]), ALU.mult)
                k3 = qkv[0:NS, 1024:2048].r("p (h d) -> p h d", d=128)
                for cch in range(8):
                    St, Sn = Sst[cch % 2], Snw[cch % 2]
                    self.load(St[:], self.st_delta[l, 2 * cch:2 * cch + 2].rearrange("s h k v -> k s h v"))
                    for sl in range(2):
                        s = 2 * cch + sl
                        self.tt("dve", kd[:, :, sl, :], k3, IDF[0:NS, s:s + 1].us(2).bc([NS, 8, 128]), ALU.mult)
                    for sl in range(2):
                        s = 2 * cch + sl
                        for half in range(2):
                            bk = self.bank()
                            self.mm(bk, [(bk[:, (h % 4) * 128:(h % 4 + 1) * 128], kd[:, h, sl, :], u[:, h * 128:(h + 1) * 128], True, True)
                                         for h in range(half * 4, half * 4 + 4)])
                            for h in range(half * 4, half * 4 + 4):
                                self.stt("dve", Sn[:, sl, h, :], St[:, sl, h, :], egbc[:, s * 8 + h:s * 8 + h + 1],
                                         bk[:, (h % 4) * 128:(h % 4 + 1) * 128], ALU.mult, ALU.add)
                    for half in range(2):
                        items = []
                        for sl in range(2):
                            s = 2 * cch + sl
                            for h in range(half * 4, half * 4 + 4):
                                items.append((bO[half][0:NS, (h % 4) * 128:(h % 4 + 1) * 128], qTd4[:, h, s, :], Sn[:, sl, h, :],
                                              s == 0 and h % 4 == 0, s == NS - 1))
                        self.mm(bO[half], items)
                    self.out_store(self.o_delta_s[l, 2 * cch:2 * cch + 2].rearrange("s h k v -> k s h v"), Sn[:])
                for half in range(2):
                    self.cp("act", oall[0:NS, half * 512:(half + 1) * 512], bO[half][0:NS, :])
                self.bank_pool = list(range(8))
                back(i, tp)
                kb.barrier()

    def sample_step_kv(self, e2, l, st_in, st_out, q_tm, k_tm, v_tm, oall, decay):
        IDF, C = self.idf, self.cst_t
        qTs = self.sb(e2, "qTs2", [128, 4, NS])
        qTd = self.sb(e2, "qTd2", [128, 4 * NS * NS])
        Sst = [self.sb(e2, "Sst2_%d" % j, [128, 2, 4, 256]) for j in range(2)]
        Snw = [self.sb(e2, "Snw2_%d" % j, [128, 2, 4, 256]) for j in range(2)]
        kd = self.sb(e2, "kd2", [NS, 4, 2, 128])
        bk = self.bank()
        self.tr(bk, [(bk[:, h * NS:(h + 1) * NS], q_tm[:, h * 128:(h + 1) * 128]) for h in range(4)], IDF[0:NS, 0:NS])
        self.cp("act", qTs[:].r("p h s -> p (h s)"), bk[:, 0:4 * NS])
        qTd4 = qTd[:].r("p (h a b) -> p h a b", a=NS, b=NS)
        self.tt("dve", qTd4, qTs[:].us(2).bc([128, 4, NS, NS]),
                C[:, C_I16:C_I16 + 256].r("p (a b) -> p a b", b=NS).us(1).bc([128, 4, NS, NS]), ALU.mult)
        k3 = k_tm.r("p (h d) -> p h d", d=128)
        self.bank_pool = [0, 1, 2, 3, 4, 5]
        bO = [self.ps[6], self.ps[7]]
        for cch in range(8):
            St, Sn = Sst[cch % 2], Snw[cch % 2]
            self.load(St[:], st_in[l, 2 * cch:2 * cch + 2].rearrange("s h k v -> k s h v"))
            for sl in range(2):
                s = 2 * cch + sl
                self.tt("dve", kd[:, :, sl, :], k3, IDF[0:NS, s:s + 1].us(2).bc([NS, 4, 128]), ALU.mult)
            for sl in range(2):
                s = 2 * cch + sl
                for half in range(2):
                    bk = self.bank()
                    self.mm(bk, [(bk[:, (h % 2) * 256:(h % 2 + 1) * 256], kd[:, h, sl, :], v_tm[:, h * 256:(h + 1) * 256], True, True)
                                 for h in range(half * 2, half * 2 + 2)])
                    for h in range(half * 2, half * 2 + 2):
                        self.stt("dve", Sn[:, sl, h, :], St[:, sl, h, :], decay(h, s), bk[:, (h % 2) * 256:(h % 2 + 1) * 256],
                                 ALU.mult, ALU.add)
            for half in range(2):
                items = []
                for sl in range(2):
                    s = 2 * cch + sl
                    for h in range(half * 2, half * 2 + 2):
                        items.append((bO[half][0:NS, (h % 2) * 256:(h % 2 + 1) * 256], qTd4[:, h, s, :], Sn[:, sl, h, :],
                                      s == 0 and h % 2 == 0, s == NS - 1))
                self.mm(bO[half], items)
            self.out_store(st_out[l, 2 * cch:2 * cch + 2].rearrange("s h k v -> k s h v"), Sn[:])
        for half in range(2):
            self.cp("act", oall[0:NS, half * 512:(half + 1) * 512], bO[half][0:NS, :])
        self.bank_pool = list(range(8))

    def mixB(self, l):
        kb = self.kb
        C = self.cst_t
        IDF = self.idf
        DTc, GIN, GOUT = C[:, C_DT:C_DT + 512], C[:, C_GIN:C_GIN + 512], C[:, C_GOUT:C_GOUT + 4]
        gam = [float(x) for x in self.gam]
        with ExitStack() as es:
            qk = self.sb(es, "qkB", [128, 1024])
            v = self.sb(es, "vB", [128, 1024])
            rp = self.sb(es, "rpB", [128, 128])
            tt_ = [self.sb(es, "ropeT%d" % j, [128, 512]) for j in range(4)]
            qkr = self.sb(es, "qkrB", [128, 1024])
            oall = self.sb(es, "oallB", [128, 1024])
            z = self.sb(es, "zB", [128, 1024])
            br = self.sb(es, "brB", [128, 1024], BF16)
            brs = self.sb(es, "brsB", [128, 8, 128], BF16)
            junk = self.sb(es, "junkB", [128, 256], BF16)
            sc3 = self.sb(es, "sc3B", [128, 16])

            def front(i, c0, tp):
                self.load(qk[0:tp, :], self.s_proj[c0:c0 + tp, O_QB:O_QB + 1024], dram=[self.bproj[i]])
                self.load(v[0:tp, :], self.s_proj[c0:c0 + tp, O_VB:O_VB + 1024], dram=[self.bproj[i]])
                self.load(rp[0:tp, :], self.rope_p[c0:c0 + tp, :] if i < 16 else self.rope_s[:, :])
                x3 = qk[0:tp, :].r("p (h d) -> p h d", d=128)
                o3 = qkr[0:tp, :].r("p (h d) -> p h d", d=128)
                x1, x2 = x3[:, :, 0:64], x3[:, :, 64:128]
                cos = rp[0:tp, 0:64].us(1).bc([tp, 8, 64])
                sin = rp[0:tp, 64:128].us(1).bc([tp, 8, 64])
                t = [a[0:tp, :].r("p (h d) -> p h d", d=64) for a in tt_]
                self.tt("dve", t[0], x1, cos, ALU.mult)
                self.tt("pool", t[1], x2, sin, ALU.mult)
                self.tt("dve", o3[:, :, 0:64], t[0], t[1], ALU.subtract)
                self.tt("pool", t[2], x2, cos, ALU.mult)
                self.tt("dve", t[3], x1, sin, ALU.mult)
                self.tt("pool", o3[:, :, 64:128], t[2], t[3], ALU.add)
                self.ts("dve", qkr[0:tp, 512:1024], qkr[0:tp, 512:1024], 128.0 ** -0.5, ALU.mult)

            def back(i, tp):
                self.gated_norm_store(i, tp, oall, O_ZB, None, 4, 256, 1, (sc3, junk, z, br, brs))

            hv = lambda x, h: x[:, h * 128:(h + 1) * 128]
            hw = lambda x, h: x[:, h * 256:(h + 1) * 256]
            with ExitStack() as e1:
                S = self.sb(e1, "SB", [128, 1024])
                qT = self.sb(e1, "qTB", [128, 512])
                kT = self.sb(e1, "kTB", [128, 512])
                PT = self.sb(e1, "PTB", [128, 512])
                qinT = self.sb(e1, "qinTB", [128, 512])
                kout = self.sb(e1, "koutB", [128, 512])
                self.memset("dve", S[:], 0.0)
                for (i, c0, tp) in self.TILES[:16]:
                    front(i, c0, tp)
                    for (dst, off) in ((qT, 0), (kT, 512)):
                        bk = self.bank()
                        self.tr(bk, [(hv(bk, h), qkr[:, off + h * 128:off + (h + 1) * 128]) for h in range(4)], IDF)
                        self.cp("act", dst[:], bk[:])
                    bk = self.bank()
                    self.mm(bk, [(hv(bk, h), hv(kT, h), hv(qT, h), True, True) for h in range(4)])
                    self.tt("dve", PT[:], bk[:], DTc, ALU.mult)
                    self.tt("pool", qinT[:], qT[:], GIN, ALU.mult)
                    self.tt("dve", kout[:].r("p (h d) -> p h d", d=128), qkr[:, 512:1024].r("p (h d) -> p h d", d=128),
                            GOUT.us(2).bc([128, 4, 128]), ALU.mult)
                    for half in range(2):
                        bk = self.bank()
                        items = []
                        for h in range(half * 2, half * 2 + 2):
                            items.append((hw(bk, h % 2), hv(PT, h), hw(v, h), True, False))
                            items.append((hw(bk, h % 2), hv(qinT, h), hw(S, h), False, True))
                        self.mm(bk, items)
                        self.cp("act", oall[:, half * 512:(half + 1) * 512], bk[:])
                    for half in range(2):
                        bk = self.bank()
                        self.mm(bk, [(hw(bk, h % 2), hv(kout, h), hw(v, h), True, True) for h in range(half * 2, half * 2 + 2)])
                        for h in range(half * 2, half * 2 + 2):
                            self.stt("dve", hw(S, h), hw(S, h), gam[h] ** 128, hw(bk, h % 2), ALU.mult, ALU.add)
                    back(i, tp)
                self.out_store(self.o_ret_p[l].rearrange("h k v -> k h v"), S[:].r("p (h v) -> p h v", v=256))
                kb.barrier()
            with ExitStack() as e2:
                i, c0, tp = self.TILES[16]
                front(i, c0, tp)
                self.sample_step_kv(e2, l, self.st_ret, self.o_ret_s, qkr[0:NS, 0:512], qkr[0:NS, 512:1024], v[0:NS, :], oall,
                                    lambda h, s: gam[h])
                back(i, tp)
                kb.barrier()

    def mixC(self, l):
        kb = self.kb
        C = self.cst_t
        IDF = self.idf
        U, ONE = C[:, C_U:C_U + 128], C[:, C_ONE:C_ONE + 128]
        with ExitStack() as es:
            qk = self.sb(es, "qkC", [128, 1024])
            v = self.sb(es, "vC", [128, 1024])
            lr = self.sb(es, "lrC", [128, 16])
            w2 = self.sb(es, "w2C", [32, 512])
            lrT = self.sb(es, "lrT", [32, 128])
            g = self.sb(es, "gC", [128, 512])
            e1_ = self.sb(es, "e1C", [128, 512])
            qt = self.sb(es, "qtC", [128, 512])
            gcb = self.sb(es, "gcb", [128, 256])
            oall = self.sb(es, "oallC", [128, 1024])
            z = self.sb(es, "zC", [128, 1024])
            br = self.sb(es, "brC", [128, 1024], BF16)
            brs = self.sb(es, "brsC", [128, 8, 128], BF16)
            junk = self.sb(es, "junkC", [128, 256], BF16)
            sc3 = self.sb(es, "sc3C", [128, 16])
            self.load(w2[0:16, :], self.w_lr2[l])
            self.load(w2[16:17, :], self.b_lr2[l:l + 1, :], partial=True)
            self.load(gcb[:], self.norm_c_g[l:l + 1, :].partition_broadcast(128))
            self.memset("dve", lrT[:], 1.0)

            def front(i, c0, tp):
                self.load(qk[0:tp, :], self.s_proj[c0:c0 + tp, O_QC:O_QC + 1024], dram=[self.bproj[i]])
                self.load(v[0:tp, :], self.s_proj[c0:c0 + tp, O_VC:O_VC + 1024], dram=[self.bproj[i]])
                self.load(lr[0:tp, :], self.s_proj[c0:c0 + tp, O_LR:O_LR + 16], dram=[self.bproj[i]])
                bk = self.bank()
                self.tr(bk, [(bk[0:16, 0:tp], lr[0:tp, 0:16])], IDF[0:tp, 0:tp])
                self.cp("act", lrT[0:16, 0:tp], bk[0:16, 0:tp])
                bk = self.bank()
                self.mm(bk, [(bk[0:tp, :], lrT[0:17, 0:tp], w2[0:17, :], True, True)])
                self.act(g[0:tp, :], bk[0:tp, :], AF.Exp, scale=-1.0)
                self.act(g[0:tp, :], g[0:tp, :], AF.Ln, bias=1.0)
                self.ts("dve", g[0:tp, :], g[0:tp, :], -1.0 / 16.0, ALU.mult)

            def back(i, tp):
                self.gated_norm_store(i, tp, oall, O_ZC, gcb, 4, 256, 2, (sc3, junk, z, br, brs))

            hv = lambda x, h: x[:, h * 128:(h + 1) * 128]
            hw = lambda x, h: x[:, h * 256:(h + 1) * 256]
            with ExitStack() as e1:
                S = self.sb(e1, "SC", [128, 1024])
                bsb = self.sb(e1, "bsbC", [128, 512])
                kt = self.sb(e1, "ktC", [128, 512])
                kout = self.sb(e1, "koutC", [128, 512])
                qtT = self.sb(e1, "qtTC", [128, 512])
                ktT = self.sb(e1, "ktTC", [128, 512])
                PT = self.sb(e1, "PTC", [128, 512])
                cdT = self.sb(e1, "cdTC", [128, 4])
                self.memset("dve", S[:], 0.0)
                for (i, c0, tp) in self.TILES[:16]:
                    front(i, c0, tp)
                    bk1 = self.bank()
                    self.mm(bk1, [(bk1[:], U, g[:], True, True)])
                    bk2 = self.bank()
                    self.mm(bk2, [(bk2[:], ONE, g[:], True, True)])
                    self.act(e1_[:], bk1[:], AF.Exp)
                    self.stt("dve", qt[:], qk[:, 0:512], 128.0 ** -0.5, e1_[:], ALU.mult, ALU.mult)
                    self.act(e1_[:], bk1[:], AF.Exp, scale=-1.0)
                    self.tt("dve", kt[:], qk[:, 512:1024], e1_[:], ALU.mult)
                    self.cp("act", bsb[:], bk1[:])
                    self.tt("dve", e1_[:], bk2[:], bsb[:], ALU.subtract)
                    self.act(e1_[:], e1_[:], AF.Exp)
                    self.tt("dve", kout[:], qk[:, 512:1024], e1_[:], ALU.mult)
                    bk3 = self.bank()
                    self.mm(bk3, [(bk3[:, h:h + 1], hv(g, h), ONE[:, 0:1], h == 0, True) for h in range(4)])
                    self.act(cdT[:], bk3[:, 0:4], AF.Exp)
                    for (dst, src) in ((qtT, qt), (ktT, kt)):
                        bk = self.bank()
                        self.tr(bk, [(hv(bk, h), hv(src, h)) for h in range(4)], IDF)
                        self.cp("act", dst[:], bk[:])
                    bk = self.bank()
                    self.mm(bk, [(hv(bk, h), hv(ktT, h), hv(qtT, h), True, True) for h in range(4)])
                    self.tt("dve", PT[:].r("p (h d) -> p h d", d=128), bk[:].r("p (h d) -> p h d", d=128),
                            U.us(1).bc([128, 4, 128]), ALU.mult)
                    for half in range(2):
                        bk = self.bank()
                        items = []
                        for h in range(half * 2, half * 2 + 2):
                            items.append((hw(bk, h % 2), hv(PT, h), hw(v, h), True, False))
                            items.append((hw(bk, h % 2), hv(qtT, h), hw(S, h), False, True))
                        self.mm(bk, items)
                        self.cp("act", oall[:, half * 512:(half + 1) * 512], bk[:])
                    for half in range(2):
                        bk = self.bank()
                        self.mm(bk, [(hw(bk, h % 2), hv(kout, h), hw(v, h), True, True) for h in range(half * 2, half * 2 + 2)])
                        for h in range(half * 2, half * 2 + 2):
                            self.stt("dve", hw(S, h), hw(S, h), cdT[:, h:h + 1], hw(bk, h % 2), ALU.mult, ALU.add)
                    back(i, tp)
                self.out_store(self.o_gla_p[l].rearrange("h k v -> k h v"), S[:].r("p (h v) -> p h v", v=256))
                kb.barrier()
            with ExitStack() as e2:
                i, c0, tp = self.TILES[16]
                egT = self.sb(e2, "egT", [128, 4 * NS])
                front(i, c0, tp)
                self.act(e1_[0:NS, :], g[0:NS, :], AF.Exp)
                bk = self.bank()
                self.tr(bk, [(bk[:, h * NS:(h + 1) * NS], e1_[0:NS, h * 128:(h + 1) * 128]) for h in range(4)], IDF[0:NS, 0:NS])
                self.cp("act", egT[:], bk[:, 0:4 * NS])
                self.ts("dve", qt[0:NS, :], qk[0:NS, 0:512], 128.0 ** -0.5, ALU.mult)
                self.sample_step_kv(e2, l, self.st_gla, self.o_gla_s, qt[0:NS, :], qk[0:NS, 512:1024], v[0:NS, :], oall,
                                    lambda h, s: egT[:, h * NS + s:h * NS + s + 1])
                back(i, tp)
                kb.barrier()

    def x_update_evac(self, es, l, first, gcol, tag):
        gP = self.sb(es, "gP" + tag, [128, D])
        gS = self.sb(es, "gS" + tag, [NS, D])
        xo = [self.sb(es, "xo%s%d" % (tag, j), [128, 512]) for j in range(4)]
        tmp = [self.sb(es, "xt%s%d" % (tag, j), [128, 512]) for j in range(2)]
        self.load(gP[:], self.s_mod[l, NS:NS + 1, gcol:gcol + D].partition_broadcast(128), dram=[self.bmod[l]])
        self.load(gS[:], self.s_mod[l, 0:NS, gcol:gcol + D], dram=[self.bmod[l]])
        cnt = [0]

        def evac(tile, blk, bk):
            i, c0, tp = tile
            c, n = blk[0], blk[1]
            x = xo[cnt[0] % 4]
            t = tmp[cnt[0] % 2]
            cnt[0] += 1
            gt = gP if i < 16 else gS
            if first and l == 0:
                src, dr = (self.xp[c0:c0 + tp, c:c + n] if i < 16 else self.xs[:, c:c + n]), []
            else:
                src, dr = self.s_x[c0:c0 + tp, c:c + n], [self.bx[i]]
            self.load(x[0:tp, 0:n], src, dram=dr)
            self.tt("dve", t[0:tp, 0:n], bk[0:tp, 0:n], gt[0:tp, c:c + n], ALU.mult)
            self.tt("pool", x[0:tp, 0:n], x[0:tp, 0:n], t[0:tp, 0:n], ALU.add)
            self.store(self.s_x[c0:c0 + tp, c:c + n], x[0:tp, 0:n], dram=[self.bx[i]])
        return evac

    def g23(self, l):
        kb = self.kb
        with ExitStack() as es:
            XT = self.sb(es, "XTm", [128, KC, NTOK], BF16)
            with ExitStack() as e:
                wsl = [self.sb(e, "g2w%d" % j, [128, 24, 512], BF16) for j in range(2)]
                bt = [self.sb(e, "g2b%d" % j, [128, 24, 128], BF16) for j in range(2)]
                gt = [self.sb(e, "g2g%d" % j, [128, 3, 512]) for j in range(2)]
                m = [self.sb(e, "g2m%d" % j, [128, 512]) for j in range(2)]
                mb = [self.sb(e, "g2mb%d" % j, [128, 512], BF16) for j in range(2)]
                tmp = [self.sb(e, "g2t%d" % j, [128, 512]) for j in range(2)]
                n = 0
                for j in range(4):
                    w = wsl[j % 2]
                    self.load(w[:], self.w_branch[l, :, j * 512:(j + 1) * 512].rearrange("(kc p) n -> p kc n", p=128), q="pool")
                    for (i, c0, tp) in self.TILES:
                        b, gg, mm_, mbb = bt[n % 2], gt[n % 2], m[n % 2], mb[n % 2]
                        n += 1
                        self.load(b[:, :, 0:tp], self.s_brT[i].rearrange("p (k c) -> p k c", c=128)[:, :, 0:tp], dram=[self.bbrT[i]])
                        self.load(gg[0:tp, :, :],
                                  self.s_proj[c0:c0 + tp, O_MG:O_MG + 3 * D].rearrange("t (n c) -> t n c", n=3)[:, :, j * 512:(j + 1) * 512],
                                  dram=[self.bproj[i]])
                        bks = []
                        for nb in range(3):
                            bk = self.bank()
                            self.mm(bk, [(bk[0:tp, :], b[:, 8 * nb + k, 0:tp], w[:, 8 * nb + k, :], k == 0, k == 7) for k in range(8)])
                            bks.append(bk)
                        self.tt("dve", mm_[0:tp, :], bks[0][0:tp, :], gg[0:tp, 0, :], ALU.mult)
                        self.tt("dve", tmp[0][0:tp, :], bks[1][0:tp, :], gg[0:tp, 1, :], ALU.mult)
                        self.tt("pool", mm_[0:tp, :], mm_[0:tp, :], tmp[0][0:tp, :], ALU.add)
                        self.tt("dve", tmp[1][0:tp, :], bks[2][0:tp, :], gg[0:tp, 2, :], ALU.mult)
                        self.tt("pool", mbb[0:tp, :], mm_[0:tp, :], tmp[1][0:tp, :], ALU.add)
                        bk = self.bank()
                        pv = bk[:].cast(BF16)
                        self.tr(bk, [(pv[:, k * 128:k * 128 + tp], mbb[0:tp, k * 128:(k + 1) * 128]) for k in range(4)],
                                self.idb[0:tp, 0:tp])
                        self.cp("act", XT[:, 4 * j:4 * j + 4, c0:c0 + tp], pv[:, 0:512].r("p (k c) -> p k c", c=128)[:, :, 0:tp])
                kb.barrier()
            with ExitStack() as e:
                evac = self.x_update_evac(e, l, True, 2 * D, "3")
                self.gemm_res(e, XT, KC, [(j * 512, 512) for j in range(4)],
                              lambda blk: self.w_out[l, :, blk[0]:blk[0] + blk[1]], evac)
                kb.barrier()

    def g4(self, l):
        kb = self.kb
        with ExitStack() as es:
            XT = self.sb(es, "XTf", [128, KC, NTOK], BF16)
            self.norm_phase(l, 2, XT)
            with ExitStack() as e:
                stg = [self.sb(e, "g4s%d" % j, [128, 512]) for j in range(4)]
                cnt = [0]

                def evac(tile, blk, bk):
                    i, c0, tp = tile
                    c, n = blk[0], blk[1]
                    s = stg[cnt[0] % 4]
                    self.cp("dve" if cnt[0] % 2 == 0 else "act", s[0:tp, 0:n], bk[0:tp, 0:n])
                    cnt[0] += 1
                    if c < DFF:
                        if i < 16:
                            self.store(self.s_ua[2 + c0:2 + c0 + tp, c:c + n], s[0:tp, 0:n], dram=[self.bua[i]])
                        else:
                            self.store(self.s_ua_s[:, c:c + n], s[0:tp, 0:n], dram=[self.bua[i]])
                    else:
                        self.store(self.s_ub[c0:c0 + tp, c - DFF:c - DFF + n], s[0:tp, 0:n], dram=[self.bub[i]])

                self.gemm_res(e, XT, KC, [(j * 512, 512) for j in range(22)],
                              lambda blk: self.w_ffn_in[l, :, blk[0]:blk[0] + blk[1]], evac)
                cs1 = self.sb(e, "fst1", [2, DFF])
                cs2 = self.sb(e, "fst2", [NS, 2, DFF])
                self.load(cs1[:], self.s_ua[SEQ:SEQ + 2, :], dram=[self.bua[15]])
                self.out_store(self.o_fconv_p[l], cs1[:])
                self.load(cs2[:, 0, :], self.st_fconv[l, :, 1, :])
                self.load(cs2[:, 1, :], self.s_ua_s[:, :], dram=[self.bua[16]], partial=True)
                self.out_store(self.o_fconv_s[l], cs2[:])
                kb.barrier()

    def ffn_elem(self, l):
        kb = self.kb
        PW = 1408
        with ExitStack() as es:
            cw = self.sb(es, "fcw", [128, 3, PW])
            cb = self.sb(es, "fcb", [128, PW])
            xa = [self.sb(es, "fxa%d" % j, [128, 3, PW]) for j in range(2)]
            xb = [self.sb(es, "fxb%d" % j, [128, PW]) for j in range(2)]
            acc = self.sb(es, "facc", [128, PW])
            tmp = self.sb(es, "ftmp", [128, PW])
            ab = self.sb(es, "fab", [128, PW], BF16)
            stg = [self.sb(es, "fstg%d" % j, [128, 11, 128], BF16) for j in range(2)]
            n = 0
            for part in range(4):
                pc = part * PW
                self.load(cw[:], self.ffn_conv_w[l:l + 1, :, pc:pc + PW].partition_broadcast(128))
                self.load(cb[:], self.ffn_conv_b[l:l + 1, pc:pc + PW].partition_broadcast(128))
                for (i, c0, tp) in self.TILES:
                    a3, b1, sg = xa[n % 2], xb[n % 2], stg[n % 2]
                    n += 1
                    for s in range(3):
                        if i < 16:
                            src, dr = self.s_ua[c0 + s:c0 + s + 128, pc:pc + PW], [self.bua[max(i - 1, 0)], self.bua[i]]
                        elif s < 2:
                            src, dr = self.st_fconv[l, :, s, pc:pc + PW], []
                        else:
                            src, dr = self.s_ua_s[:, pc:pc + PW], [self.bua[16]]
                        self.load(a3[0:tp, s, :], src, dram=dr, partial=(s > 0))
                    self.load(b1[0:tp, :], self.s_ub[c0:c0 + tp, pc:pc + PW], dram=[self.bub[i]])
                    self.tt("pool", acc[0:tp, :], a3[0:tp, 0, :], cw[0:tp, 0, :], ALU.mult)
                    self.tt("pool", tmp[0:tp, :], a3[0:tp, 1, :], cw[0:tp, 1, :], ALU.mult)
                    self.tt("dve", acc[0:tp, :], acc[0:tp, :], tmp[0:tp, :], ALU.add)
                    self.tt("pool", tmp[0:tp, :], a3[0:tp, 2, :], cw[0:tp, 2, :], ALU.mult)
                    self.tt("dve", acc[0:tp, :], acc[0:tp, :], tmp[0:tp, :], ALU.add)
                    self.tt("dve", acc[0:tp, :], acc[0:tp, :], cb[0:tp, :], ALU.add)
                    self.act(acc[0:tp, :], acc[0:tp, :], AF.Silu)
                    self.tt("dve", ab[0:tp, :], acc[0:tp, :], b1[0:tp, :], ALU.mult)
                    for (k0, nk) in ((0, 8), (8, 3)):
                        bk = self.bank()
                        pv = bk[:].cast(BF16)
                        self.tr(bk, [(pv[:, k * 128:k * 128 + tp], ab[0:tp, (k0 + k) * 128:(k0 + k + 1) * 128]) for k in range(nk)],
                                self.idb[0:tp, 0:tp])
                        self.cp("act", sg[:, k0:k0 + nk, 0:tp], pv[:, 0:nk * 128].r("p (k c) -> p k c", c=128)[:, :, 0:tp])
                    dst = self.s_actT[i].rearrange("p (k c) -> p k c", c=128)[:, part * 11:(part + 1) * 11, 0:tp]
                    self.store(dst, sg[:, :, 0:tp], dram=[self.bactT[i]])
            kb.barrier()

    def g5(self, l):
        kb = self.kb
        with ExitStack() as es:
            wsl = [self.sb(es, "g5w%d" % j, [128, 44, 512], BF16) for j in range(2)]
            at = [self.sb(es, "g5a%d" % j, [128, 44, 128], BF16) for j in range(2)]
            evac = self.x_update_evac(es, l, False, 5 * D, "5")
            n = 0
            for j in range(4):
                w = wsl[j % 2]
                self.load(w[:], self.w_ffn_out[l, :, j * 512:(j + 1) * 512].rearrange("(kc p) n -> p kc n", p=128), q="pool")
                for (i, c0, tp) in self.TILES:
                    a = at[n % 2]
                    n += 1
                    self.load(a[:, :, 0:tp], self.s_actT[i].rearrange("p (k c) -> p k c", c=128)[:, :, 0:tp], dram=[self.bactT[i]])
                    bk = self.bank()
                    self.mm(bk, [(bk[0:tp, :], a[:, k, 0:tp], w[:, k, :], k == 0, k == 43) for k in range(44)])
                    evac((i, c0, tp), (j * 512, 512), bk)
            kb.barrier()

    def layer(self, l):
        import os
        mx = os.environ.get("MIX", "ABC")
        with ExitStack() as es:
            XT = self.sb(es, "XT", [128, KC, NTOK], BF16)
            self.norm_phase(l, 1, XT)
            self.g1(l, XT)
        if self.stop == "g1":
            return
        if "A" in mx:
            self.mixA(l)
        if "B" in mx:
            self.mixB(l)
        if "C" in mx:
            self.mixC(l)
        if self.stop == "mix":
            return
        self.g23(l)
        if self.stop == "g3":
            return
        self.g4(l)
        self.ffn_elem(l)
        self.g5(l)

    def final_norm(self):
        kb = self.kb
        with ExitStack() as es:
            gt = self.sb(es, "fng", [128, D])
            xsl = [self.sb(es, "fnx%d" % i, [128, D]) for i in range(2)]
            jk = self.sb(es, "fnj", [128, D], BF16)
            ssq = [self.sb(es, "fnq%d" % i, [128, 2]) for i in range(2)]
            self.load(gt[:], self.final_g[0:1, :].partition_broadcast(128))
            for (i, c0, tp) in self.TILES:
                xt, sq = xsl[i % 2], ssq[i % 2]
                self.load(xt[0:tp, :], self.s_x[c0:c0 + tp, :], dram=[self.bx[i]])
                self.memset("dve", sq[0:tp, :], 0.0)
                self.act(jk[0:tp, :], xt[0:tp, :], AF.Square, accum=sq[0:tp, 0:1])
                self.act(sq[0:tp, 1:2], sq[0:tp, 0:1], AF.Sqrt, bias=EPS, scale=1.0 / D)
                self.recip(sq[0:tp, 1:2], sq[0:tp, 1:2])
                self.stt("dve", xt[0:tp, :], xt[0:tp, :], sq[0:tp, 1:2], gt[0:tp, :], ALU.mult, ALU.mult)
                self.out_store(self.y_p[c0:c0 + tp, :] if i < 16 else self.y_s[:, :], xt[0:tp, :])
            kb.barrier()


_W_KEYS = ["w_ada", "b_ada", "norm1_g", "w_in", "conv_a_w", "a_log", "dt_bias", "norm_a_g", "w_lr2", "b_lr2",
           "norm_c_g", "w_out", "norm2_g", "w_ffn_in", "ffn_conv_w", "ffn_conv_b", "w_ffn_out"]


def _core_inputs(inp, c, shared):
    p = c % 4
    s0 = c * NS
    m = dict(shared)
    m["xp"] = inp["x_prompt"][p]
    m["xs"] = inp["x_sample"][s0:s0 + NS, 0, :]
    m["c17"] = np.concatenate([inp["c_sample"][s0:s0 + NS], inp["c_prompt"][p:p + 1]], 0)
    m["st_delta"] = inp["state_delta"][:, s0:s0 + NS]
    m["st_dconv"] = inp["state_delta_conv"][:, s0:s0 + NS]
    m["st_ret"] = inp["state_ret"][:, s0:s0 + NS]
    m["st_gla"] = inp["state_gla"][:, s0:s0 + NS]
    m["st_fconv"] = inp["state_ffn_conv"][:, s0:s0 + NS]
    return {k: np.ascontiguousarray(v, dtype=np.float32) for k, v in m.items()}


def kernel(**inp):
    inp = {k: np.asarray(v) for k, v in inp.items()}
    cst, _ = _consts()
    rp, rs = _rope_tables()
    shared = {k: np.ascontiguousarray(inp[k], dtype=np.float32) for k in _W_KEYS}
    shared["w_branch"] = np.ascontiguousarray(inp["w_branch"], dtype=np.float32).reshape(DEPTH, 3072, D)
    shared["final_g"] = np.ascontiguousarray(inp["final_norm_g"], dtype=np.float32)[None, :]
    shared["cst"] = cst
    shared["rope_p"] = rp
    shared["rope_s"] = rs
    P = Prog()
    nc = P.build()
    in_maps = [_core_inputs(inp, c, shared) for c in range(8)]
    res = run_bass_kernel_spmd(nc, in_maps, core_ids=list(range(8)))
    r = res.results
    B = 4
    y_p = np.stack([r[c]["y_p"] for c in range(B)], 0)
    y_s = np.concatenate([r[c]["y_s"] for c in range(8)], 0)[:, None, :]

    def pstack(name):
        return np.stack([r[c][name] for c in range(B)], 1)

    def sstack(name):
        return np.concatenate([r[c][name] for c in range(8)], 1)

    outs = (y_p, y_s, pstack("o_delta_p"), pstack("o_dconv_p"), pstack("o_ret_p"), pstack("o_gla_p"), pstack("o_fconv_p"),
            sstack("o_delta_s"), sstack("o_dconv_s"), sstack("o_ret_s"), sstack("o_gla_s"), sstack("o_fconv_s"))
    return tuple(np.ascontiguousarray(o, dtype=np.float32) for o in outs)
```

```python
import numpy as np
from contextlib import ExitStack
import concourse.bass as bass
import concourse.mybir as mybir
from concourse.bass_utils import run_bass_kernel_spmd

F32 = mybir.dt.float32
BF16 = mybir.dt.bfloat16
AF = mybir.ActivationFunctionType
ALU = mybir.AluOpType
AX = mybir.AxisListType

D = 2048
SEQ = 2048
DEPTH = 4
NS = 16
NTOK = SEQ + NS
KC = D // 128
HA, HB, HC = 8, 4, 4
DFF = 5632
D_IN = 16416
EPS = 1e-6
O_QKVA, O_ZA, O_BETA, O_DEC = 0, 3072, 4096, 4104
O_QB, O_KB, O_VB, O_ZB = 4112, 4624, 5136, 6160
O_QC, O_KC, O_VC, O_ZC, O_LR, O_MG = 7184, 7696, 8208, 9232, 10256, 10272


class Ev:
    __slots__ = ("sem", "sid", "val")

    def __init__(self, sem, sid, val):
        self.sem = sem
        self.sid = sid
        self.val = val


class Buf:
    __slots__ = ("name", "w", "r", "dsem", "dcnt")

    def __init__(self, name=""):
        self.name = name
        self.w = []
        self.r = []
        self.dsem = None
        self.dcnt = 0


def _compact(evs):
    best = {}
    for ev in evs:
        o = best.get(ev.sid)
        if o is None or ev.val > o.val:
            best[ev.sid] = ev
    return list(best.values())


class KB:
    def __init__(self, nc, same_engine_sync=True):
        self.nc = nc
        self.engs = {"pe": nc.tensor, "act": nc.scalar, "dve": nc.vector, "pool": nc.gpsimd, "sp": nc.sync}
        self.sems = {}
        self.cnt = {}
        self.seen = {k: {} for k in self.engs}
        self.nsem = 0
        self.dma_bufs = []
        for k in self.engs:
            self.sems[k] = self._sem("e_" + k)
            self.cnt[k] = 0
        self.same = same_engine_sync
        self.n_ins = 0
        self.n_wait = 0
        self.free_dsems = []

    def _sem(self, name):
        self.nsem += 1
        s = self.nc.alloc_semaphore(name + "_%d" % self.nsem)
        return (s, self.nsem)

    def _wait(self, eng, ev):
        seen = self.seen[eng]
        if seen.get(ev.sid, 0) >= ev.val:
            return
        self.engs[eng].wait_ge(ev.sem, ev.val)
        seen[ev.sid] = ev.val
        self.n_wait += 1

    def _deps(self, eng, reads, writes, extra, pw=()):
        evs = []
        for b in reads:
            evs.extend(b.w)
        for b in writes:
            evs.extend(b.w)
            evs.extend(b.r)
        for b in pw:
            evs.extend(b.r)
        if extra:
            evs.extend(extra)
        own = self.sems[eng][1]
        for ev in evs:
            if ev.sid == own and (eng == "pe" or not self.same):
                continue
            self._wait(eng, ev)

    def _commit(self, ev, reads, writes, pw=()):
        for b in pw:
            b.w.append(ev)
        for b in reads:
            b.r.append(ev)
            if len(b.r) > 16:
                b.r = _compact(b.r)
        for b in pw:
            if len(b.w) > 16:
                b.w = _compact(b.w)
        for b in writes:
            b.w = [ev]
            b.r = []

    def op(self, eng, fn, reads=(), writes=(), extra=None):
        self._deps(eng, reads, writes, extra)
        ins = fn(self.engs[eng])
        sem, sid = self.sems[eng]
        ins.then_inc(sem, 1)
        self.cnt[eng] += 1
        ev = Ev(sem, sid, self.cnt[eng])
        self._commit(ev, reads, writes)
        self.n_ins += 1
        return ev

    def ops(self, eng, fns, reads=(), writes=(), extra=None):
        self._deps(eng, reads, writes, extra)
        e = self.engs[eng]
        ins = None
        for fn in fns:
            ins = fn(e)
            self.n_ins += 1
        sem, sid = self.sems[eng]
        ins.then_inc(sem, 1)
        self.cnt[eng] += 1
        ev = Ev(sem, sid, self.cnt[eng])
        self._commit(ev, reads, writes)
        return ev

    def dma(self, q, out, in_, tag, reads=(), writes=(), pw=(), extra=None, **kw):
        self._deps(q, reads, writes, extra, pw)
        t = tag
        if t.dsem is None:
            if self.free_dsems:
                t.dsem = self.free_dsems.pop()
            else:
                s = self._sem("d")
                t.dsem = [s[0], s[1], 0]
            t.dcnt = t.dsem[2]
            self.dma_bufs.append(t)
        ins = self.engs[q].dma_start(out=out, in_=in_, **kw)
        ins.then_inc(t.dsem[0], 16)
        t.dcnt += 16
        t.dsem[2] = t.dcnt
        ev = Ev(t.dsem[0], t.dsem[1], t.dcnt)
        self._commit(ev, reads, writes, pw)
        self.n_ins += 1
        return ev

    def barrier(self, release_all=True):
        for b in self.dma_bufs:
            self._wait("sp", Ev(b.dsem[0], b.dsem[1], b.dcnt))
        for o in ("pe", "act", "dve", "pool"):
            if self.cnt[o]:
                self._wait("sp", Ev(self.sems[o][0], self.sems[o][1], self.cnt[o]))
        ins = self.engs["sp"].nop()
        sem, sid = self.sems["sp"]
        ins.then_inc(sem, 1)
        self.cnt["sp"] += 1
        ev = Ev(sem, sid, self.cnt["sp"])
        for o in ("pe", "act", "dve", "pool"):
            self._wait(o, ev)
        for e in self.engs:
            seen = self.seen[e]
            for o in self.engs:
                seen[self.sems[o][1]] = self.cnt[o]
            for b in self.dma_bufs:
                seen[b.dsem[1]] = b.dcnt
            for d in self.free_dsems:
                seen[d[1]] = d[2]
        if release_all:
            for b in self.dma_bufs:
                self.free_dsems.append(b.dsem)
                b.dsem = None
            self.dma_bufs = []
        return ev


class V:
    __slots__ = ("ap", "b")

    def __init__(self, ap, b):
        self.ap = ap
        self.b = b

    def __getitem__(self, k):
        return V(self.ap[k], self.b)

    def r(self, pat, **kw):
        return V(self.ap.rearrange(pat, **kw), self.b)

    def bc(self, shape):
        return V(self.ap.to_broadcast(list(shape)), self.b)

    def us(self, ax):
        return V(self.ap.unsqueeze(ax), self.b)

    def cast(self, dt):
        return V(self.ap.bitcast(dt), self.b)


class T:
    __slots__ = ("t", "b")

    def __init__(self, t, name):
        self.t = t
        self.b = Buf(name)

    def __getitem__(self, k):
        return V(self.t[k], self.b)


C_ID, C_U, C_SL, C_ONE, C_MS, C_I16, C_DT, C_GIN, C_GOUT = 0, 128, 256, 384, 512, 640, 896, 1408, 1920
C_BD16, C_ML32, C_ML64, C_ML128, C_END = 1928, 2056, 2184, 2312, 2440


def _consts():
    c = np.zeros((128, C_END), np.float32)
    p = np.arange(128)
    c[:, C_ID:C_ID + 128] = np.eye(128)
    c[:, C_U:C_U + 128] = (p[:, None] <= p[None, :])
    c[:, C_SL:C_SL + 128] = (p[:, None] > p[None, :])
    c[:, C_ONE:C_ONE + 128] = 1.0
    c[:, C_MS:C_MS + 128] = (p[:, None] < p[None, :])
    c[:, C_I16:C_I16 + 256] = np.eye(16).reshape(1, 256)
    bd = lambda n: ((p[:, None] // n) == (p[None, :] // n)).astype(np.float32)
    c[:, C_BD16:C_BD16 + 128] = bd(16)
    c[:, C_ML32:C_ML32 + 128] = bd(32) - bd(16)
    c[:, C_ML64:C_ML64 + 128] = bd(64) - bd(32)
    c[:, C_ML128:C_ML128 + 128] = 1.0 - bd(64)
    gam = 1.0 - np.exp2(-5.0 - np.arange(HB, dtype=np.float64))
    for h in range(HB):
        dt = np.where(p[:, None] <= p[None, :], gam[h] ** np.maximum(p[None, :] - p[:, None], 0), 0.0)
        c[:, C_DT + h * 128:C_DT + (h + 1) * 128] = dt
        c[:, C_GIN + h * 128:C_GIN + (h + 1) * 128] = (gam[h] ** (p + 1.0))[None, :]
        c[:, C_GOUT + h] = gam[h] ** (127.0 - p)
    return c, gam


def _rope_tables():
    half = 64
    inv = (np.float32(10000.0) ** (-np.arange(half, dtype=np.float32) / np.float32(half))).astype(np.float32)
    pos_p = np.arange(SEQ, dtype=np.float32)
    ang = (pos_p[:, None] * inv[None, :]).astype(np.float32)
    rp = np.concatenate([np.cos(ang), np.sin(ang)], axis=1).astype(np.float32)
    angs = (np.float32(16384.0) * inv).astype(np.float32)
    rs = np.concatenate([np.cos(angs), np.sin(angs)])[None, :].repeat(NS, 0).astype(np.float32)
    return rp, rs


class Prog:
    def __init__(self, depth=DEPTH, debug=()):
        self.depth = depth
        self.debug = set(debug)
        nc = bass.Bass("TRN2", target_bir_lowering=False)
        self.nc = nc
        import os
        self.kb = KB(nc, same_engine_sync=(os.environ.get('SAME_SYNC', '1') == '1'))
        self.inputs = {}
        self.outputs = {}
        self.bank_rr = 0
        self.uid = 0
        self.stop = None
        self.bank_pool = list(range(8))
        self.gam = _consts()[1]
        self._declare()

    def din(self, name, shape):
        t = self.nc.dram_tensor(name, list(shape), F32, kind="ExternalInput").ap()
        self.inputs[name] = t
        return t

    def dout(self, name, shape, dt=F32):
        t = self.nc.dram_tensor(name, list(shape), dt, kind="ExternalOutput").ap()
        self.outputs[name] = t
        return t

    def dscr(self, name, shape, dt=F32):
        kind = "ExternalOutput" if name in self.debug else "Internal"
        t = self.nc.dram_tensor(name, list(shape), dt, kind=kind).ap()
        if name in self.debug:
            self.outputs[name] = t
        return t

    def _declare(self):
        L = self.depth
        self.xp = self.din("xp", [SEQ, D])
        self.xs = self.din("xs", [NS, D])
        self.c17 = self.din("c17", [NS + 1, D])
        self.st_delta = self.din("st_delta", [L, NS, HA, 128, 128])
        self.st_dconv = self.din("st_dconv", [L, NS, 3, 3072])
        self.st_ret = self.din("st_ret", [L, NS, HB, 128, 256])
        self.st_gla = self.din("st_gla", [L, NS, HC, 128, 256])
        self.st_fconv = self.din("st_fconv", [L, NS, 2, DFF])
        self.w_ada = self.din("w_ada", [L, D, 6 * D])
        self.b_ada = self.din("b_ada", [L, 6 * D])
        self.norm1_g = self.din("norm1_g", [L, D])
        self.w_in = self.din("w_in", [L, D, D_IN])
        self.conv_a_w = self.din("conv_a_w", [L, 4, 3072])
        self.a_log = self.din("a_log", [L, HA])
        self.dt_bias = self.din("dt_bias", [L, HA])
        self.norm_a_g = self.din("norm_a_g", [L, 128])
        self.w_lr2 = self.din("w_lr2", [L, 16, 512])
        self.b_lr2 = self.din("b_lr2", [L, 512])
        self.norm_c_g = self.din("norm_c_g", [L, 256])
        self.w_branch = self.din("w_branch", [L, 3072, D])
        self.w_out = self.din("w_out", [L, D, D])
        self.norm2_g = self.din("norm2_g", [L, D])
        self.w_ffn_in = self.din("w_ffn_in", [L, D, 2 * DFF])
        self.ffn_conv_w = self.din("ffn_conv_w", [L, 3, DFF])
        self.ffn_conv_b = self.din("ffn_conv_b", [L, DFF])
        self.w_ffn_out = self.din("w_ffn_out", [L, DFF, D])
        self.final_g = self.din("final_g", [1, D])
        self.cst = self.din("cst", [128, C_END])
        self.rope_p = self.din("rope_p", [SEQ, 128])
        self.rope_s = self.din("rope_s", [NS, 128])
        self.y_p = self.dout("y_p", [SEQ, D])
        self.y_s = self.dout("y_s", [NS, D])
        self.o_delta_p = self.dout("o_delta_p", [L, HA, 128, 128])
        self.o_dconv_p = self.dout("o_dconv_p", [L, 3, 3072])
        self.o_ret_p = self.dout("o_ret_p", [L, HB, 128, 256])
        self.o_gla_p = self.dout("o_gla_p", [L, HC, 128, 256])
        self.o_fconv_p = self.dout("o_fconv_p", [L, 2, DFF])
        self.o_delta_s = self.dout("o_delta_s", [L, NS, HA, 128, 128])
        self.o_dconv_s = self.dout("o_dconv_s", [L, NS, 3, 3072])
        self.o_ret_s = self.dout("o_ret_s", [L, NS, HB, 128, 256])
        self.o_gla_s = self.dout("o_gla_s", [L, NS, HC, 128, 256])
        self.o_fconv_s = self.dout("o_fconv_s", [L, NS, 2, DFF])
        self.s_x = self.dscr("s_x", [NTOK, D])
        self.s_mod = self.dscr("s_mod", [L, NS + 1, 6 * D])
        self.s_qkva = self.dscr("s_qkva", [3 + SEQ, 3072])
        self.s_qkva_s = self.dscr("s_qkva_s", [NS, 3072])
        self.s_proj = self.dscr("s_proj", [NTOK, D_IN])
        self.s_brT = self.dscr("s_brT", [17, 128, 24 * 128], BF16)
        self.s_ua = self.dscr("s_ua", [2 + SEQ, DFF])
        self.s_ua_s = self.dscr("s_ua_s", [NS, DFF])
        self.s_ub = self.dscr("s_ub", [NTOK, DFF])
        self.s_actT = self.dscr("s_actT", [17, 128, 44 * 128], BF16)

    def sb(self, es, name, shape, dt=F32):
        self.uid += 1
        name = "%s_%d" % (name, self.uid)
        t = es.enter_context(self.nc.sbuf_tensor(name, list(shape), dt))
        return T(t, name)

    @staticmethod
    def _a(x):
        return x.ap if isinstance(x, V) else x

    @staticmethod
    def _bufs(*xs):
        out = []
        for x in xs:
            if isinstance(x, V) and x.b not in out:
                out.append(x.b)
        return out

    def tt(self, eng, out, a, b, op):
        return self.kb.op(eng, lambda e: e.tensor_tensor(out=out.ap, in0=a.ap, in1=b.ap, op=op),
                          reads=self._bufs(a, b), writes=[out.b])

    def ts(self, eng, out, a, s1, op0, s2=None, op1=None, accum=None):
        kw = {}
        if op1 is not None:
            kw["op1"] = op1
        if accum is not None:
            kw["accum_out"] = accum.ap
        return self.kb.op(eng, lambda e: e.tensor_scalar(out=out.ap, in0=a.ap, scalar1=self._a(s1), scalar2=self._a(s2),
                                                         op0=op0, **kw),
                          reads=self._bufs(a, s1, s2), writes=self._bufs(out, accum))

    def stt(self, eng, out, a, s, b, op0, op1):
        return self.kb.op(eng, lambda e: e.scalar_tensor_tensor(out=out.ap, in0=a.ap, scalar=self._a(s), in1=b.ap,
                                                                op0=op0, op1=op1),
                          reads=self._bufs(a, s, b), writes=[out.b])

    def act(self, out, a, func, bias=None, scale=None, accum=None):
        kw = {}
        if bias is not None:
            kw["bias"] = self._a(bias)
        if scale is not None:
            kw["scale"] = self._a(scale)
        if accum is not None:
            kw["accum_out"] = accum.ap
        return self.kb.op("act", lambda e: e.activation(out=out.ap, in_=a.ap, func=func, **kw),
                          reads=self._bufs(a, bias, scale), writes=self._bufs(out, accum))

    def cp(self, eng, out, a):
        if eng == "act":
            return self.act(out, a, AF.Copy)
        return self.kb.op(eng, lambda e: e.tensor_copy(out=out.ap, in_=a.ap), reads=[a.b], writes=[out.b])

    def memset(self, eng, out, val):
        return self.kb.op(eng, lambda e: e.memset(out.ap, val), writes=[out.b])

    def recip(self, out, a):
        return self.kb.op("dve", lambda e: e.reciprocal(out=out.ap, in_=a.ap), reads=[a.b], writes=[out.b])

    def mm(self, out_bank, items):
        fns = []
        rd = []
        for (o, l, r, st, sp) in items:
            fns.append(lambda e, o=o, l=l, r=r, st=st, sp=sp: e.matmul(o.ap, lhsT=l.ap, rhs=r.ap, start=st, stop=sp))
            for x in (l, r):
                if x.b not in rd:
                    rd.append(x.b)
        return self.kb.ops("pe", fns, reads=rd, writes=[out_bank.b])

    def tr(self, out_bank, items, ident):
        fns = []
        rd = [ident.b]
        for (o, a) in items:
            fns.append(lambda e, o=o, a=a: e.transpose(out=o.ap, in_=a.ap, identity=ident.ap))
            if a.b not in rd:
                rd.append(a.b)
        return self.kb.ops("pe", fns, reads=rd, writes=[out_bank.b])

    def load(self, dst, src_ap, dram=(), q="sp", partial=False, **kw):
        if partial:
            return self.kb.dma(q, dst.ap, src_ap, tag=dst.b, reads=list(dram), pw=[dst.b], **kw)
        return self.kb.dma(q, dst.ap, src_ap, tag=dst.b, reads=list(dram), writes=[dst.b], **kw)

    def store(self, dst_ap, src, dram=(), q="sp", **kw):
        return self.kb.dma(q, dst_ap, src.ap, tag=src.b, reads=[src.b], pw=list(dram), **kw)

    def d2d(self, dst_ap, src_ap, tag, rd=(), wr=()):
        return self.kb.dma("sp", dst_ap, src_ap, tag=tag, reads=list(rd), pw=list(wr))

    def bank(self):
        pool = self.bank_pool
        b = self.ps[pool[self.bank_rr % len(pool)]]
        self.bank_rr += 1
        return b

    TILES = [(i, i * 128, 128) for i in range(16)] + [(16, SEQ, NS)]

    def xrows(self, l, i, c0, tp):
        if l == 0:
            return (self.xp[c0:c0 + tp, :] if i < 16 else self.xs[:, :]), None
        return self.s_x[c0:c0 + tp, :], self.bx[i]

    def build(self):
        nc, kb = self.nc, self.kb
        self.ps = [T(nc.alloc_psum_tensor("ps%d" % i, [128, 512], F32), "ps%d" % i) for i in range(8)]
        self.bx = [Buf("x%d" % i) for i in range(17)]
        self.bmod = [Buf("mod%d" % l) for l in range(self.depth)]
        self.bqkva = [Buf("qkva%d" % i) for i in range(17)]
        self.bproj = [Buf("proj%d" % i) for i in range(17)]
        self.bbrT = [Buf("brT%d" % i) for i in range(17)]
        self.bua = [Buf("ua%d" % i) for i in range(17)]
        self.bub = [Buf("ub%d" % i) for i in range(17)]
        self.bactT = [Buf("actT%d" % i) for i in range(17)]
        self.bout = Buf("outs")
        self.out_evs = []
        with ExitStack() as g:
            self.cst_t = self.sb(g, "cst_sb", [128, C_END])
            self.idb = self.sb(g, "idb", [128, 128], BF16)
            self.cT = self.sb(g, "cT", [128, KC, NS + 1], BF16)
            self.load(self.cst_t[:], self.cst)
            self.cp("dve", self.idb[:], self.cst_t[:, C_ID:C_ID + 128])
            self.idf = self.cst_t[:, C_ID:C_ID + 128]
            self.prologue()
            if self.stop not in ("pro0", "pro"):
                for l in range(self.depth):
                    self.layer(l)
                self.final_norm()
            kb.barrier()
        return nc

    def prologue(self):
        kb = self.kb
        with ExitStack() as es:
            z = self.sb(es, "zrow", [3, DFF])
            self.memset("dve", z[:], 0.0)
            self.store(self.s_qkva[0:3, :], z[:, 0:3072], dram=[self.bqkva[0]])
            self.store(self.s_ua[0:2, :], z[0:2, :], dram=[self.bua[0]])
            c = self.sb(es, "c17_sb", [NS + 1, D])
            cb = self.sb(es, "c17b", [NS + 1, D], BF16)
            self.load(c[:], self.c17)
            self.act(cb[:], c[:], AF.Silu)
            for half in range(2):
                bk = self.bank()
                pv = bk[:].cast(BF16)
                self.tr(bk, [(pv[:, k * 32:k * 32 + 17], cb[:, (half * 8 + k) * 128:(half * 8 + k + 1) * 128]) for k in range(8)],
                        self.idb[0:NS + 1, 0:NS + 1])
                self.cp("act", self.cT[:, half * 8:half * 8 + 8, :], pv[:, 0:256].r("p (k c) -> p k c", c=32)[:, :, 0:17])
            if self.stop == "pro0":
                self.store(self.s_mod[0, :, 0:16 * 17].rearrange("a (k c) -> a k c", c=17)[0:1].rearrange("a k c -> (a k) c"), self.cT[0:16, 0, :].r("p c -> p c"), dram=[self.bmod[0]]) if False else None
                kb.barrier()
                return
            wsl = [self.sb(es, "wada%d" % i, [128, KC, 512], BF16) for i in range(2)]
            bsl = [self.sb(es, "bada%d" % i, [NS + 1, 512]) for i in range(2)]
            osl = [self.sb(es, "moda%d" % i, [NS + 1, 512]) for i in range(2)]
            n = 0
            for l in range(self.depth):
                for j in range(24):
                    w = wsl[n % 2]
                    self.load(w[:], self.w_ada[l, :, j * 512:(j + 1) * 512].rearrange("(kc p) n -> p kc n", p=128), q="pool")
                    bb = bsl[n % 2]
                    self.load(bb[:], self.b_ada[l:l + 1, j * 512:(j + 1) * 512].partition_broadcast(NS + 1))
                    bk = self.bank()
                    self.mm(bk, [(bk[0:NS + 1, :], self.cT[:, k, :], w[:, k, :], k == 0, k == KC - 1) for k in range(KC)])
                    o = osl[n % 2]
                    self.tt("dve", o[:], bk[0:NS + 1, :], bb[:], ALU.add)
                    self.store(self.s_mod[l, :, j * 512:(j + 1) * 512], o[:], dram=[self.bmod[l]])
                    n += 1
            kb.barrier()

    def xsrc(self, l, which, i, c0, tp):
        if which == 1 and l == 0:
            return (self.xp[c0:c0 + tp, :] if i < 16 else self.xs[:, :]), []
        return self.s_x[c0:c0 + tp, :], [self.bx[i]]

    def norm_phase(self, l, which, XT):
        kb = self.kb
        gam = self.norm1_g if which == 1 else self.norm2_g
        o_sh, o_sc = (0, D) if which == 1 else (3 * D, 4 * D)
        with ExitStack() as es:
            gsP = self.sb(es, "gsP", [128, D])
            shP = self.sb(es, "shP", [128, D])
            gsS = self.sb(es, "gsS", [NS, D])
            shS = self.sb(es, "shS", [NS, D])
            gt = self.sb(es, "gt", [128, D])
            xsl = [self.sb(es, "nx%d" % i, [128, D]) for i in range(2)]
            hsl = [self.sb(es, "nh%d" % i, [128, D], BF16) for i in range(2)]
            ssq = [self.sb(es, "nssq%d" % i, [128, 2]) for i in range(2)]
            self.load(gt[:], gam[l:l + 1, :].partition_broadcast(128))
            self.load(gsP[:], self.s_mod[l, NS:NS + 1, o_sc:o_sc + D].partition_broadcast(128), dram=[self.bmod[l]])
            self.load(shP[:], self.s_mod[l, NS:NS + 1, o_sh:o_sh + D].partition_broadcast(128), dram=[self.bmod[l]])
            self.load(gsS[:], self.s_mod[l, 0:NS, o_sc:o_sc + D], dram=[self.bmod[l]])
            self.load(shS[:], self.s_mod[l, 0:NS, o_sh:o_sh + D], dram=[self.bmod[l]])
            self.stt("dve", gsP[:], gsP[:], 1.0, gt[:], ALU.add, ALU.mult)
            self.stt("dve", gsS[:], gsS[:], 1.0, gt[0:NS, :], ALU.add, ALU.mult)
            for (i, c0, tp) in self.TILES:
                xt, hb, sq = xsl[i % 2], hsl[i % 2], ssq[i % 2]
                gs, sh = (gsP, shP) if i < 16 else (gsS, shS)
                src, dr = self.xsrc(l, which, i, c0, tp)
                self.load(xt[0:tp, :], src, dram=dr)
                self.memset("dve", sq[0:tp, :], 0.0)
                self.act(hb[0:tp, :], xt[0:tp, :], AF.Square, accum=sq[0:tp, 0:1])
                self.act(sq[0:tp, 1:2], sq[0:tp, 0:1], AF.Sqrt, bias=EPS, scale=1.0 / D)
                self.recip(sq[0:tp, 1:2], sq[0:tp, 1:2])
                self.stt("dve", xt[0:tp, :], xt[0:tp, :], sq[0:tp, 1:2], gs[0:tp, :], ALU.mult, ALU.mult)
                self.tt("dve", hb[0:tp, :], xt[0:tp, :], sh[0:tp, :], ALU.add)
                for half in range(2):
                    bk = self.bank()
                    pv = bk[:].cast(BF16)
                    self.tr(bk, [(pv[:, k * 128:k * 128 + tp], hb[0:tp, (half * 8 + k) * 128:(half * 8 + k + 1) * 128])
                                 for k in range(8)], self.idb[0:tp, 0:tp])
                    self.cp("act" if half == 0 else "dve", XT[:, half * 8:half * 8 + 8, c0:c0 + tp],
                            pv.r("p (k c) -> p k c", c=128)[:, :, 0:tp])
            kb.barrier()

    G1_SEGS = [(0, 3072, "copy"), (3072, 1024, "silu"), (4096, 16, "copy"), (4112, 1024, "copy"), (5136, 1024, "copy"),
               (6160, 1024, "silu"), (7184, 1024, "copy"), (8208, 1024, "copy"), (9232, 1024, "silu"),
               (10256, 16, "copy"), (10272, 6144, "sigmoid")]

    def gemm_res(self, es, XT, kc, blocks, wsrc, evac, tiles=None, nw=2, wn=512, pre=None):
        wsl = [self.sb(es, "gw%d" % i, [128, kc, wn], BF16) for i in range(nw)]
        tiles = self.TILES if tiles is None else tiles
        its = [(jb, blk, t) for jb, blk in enumerate(blocks) for t in tiles]
        if pre is not None:
            pre(its[0][2], its[0][1])
        for n, (jb, blk, (i, c0, tp)) in enumerate(its):
            w = wsl[jb % nw]
            nn = blk[1]
            if (i, c0, tp) == tiles[0]:
                self.load(w[:, :, 0:nn], wsrc(blk).rearrange("(kc p) n -> p kc n", p=128), q="pool")
            if pre is not None and n + 1 < len(its):
                pre(its[n + 1][2], its[n + 1][1])
            bk = self.bank()
            self.mm(bk, [(bk[0:tp, 0:nn], XT[:, k, c0:c0 + tp], w[:, k, 0:nn], k == 0, k == kc - 1) for k in range(kc)])
            evac((i, c0, tp), blk, bk)

    def g1(self, l, XT):
        kb = self.kb
        blocks = []
        for (c, n, f) in self.G1_SEGS:
            for o in range(0, n, 512):
                blocks.append((c + o, min(512, n - o), f))
        import os
        if os.environ.get("G1_BLK"):
            a, b = os.environ["G1_BLK"].split(":")
            blocks = blocks[int(a):int(b)]
        with ExitStack() as es:
            stg = [self.sb(es, "g1s%d" % i, [128, 512]) for i in range(4)]
            cnt = [0]

            def evac(tile, blk, bk):
                i, c0, tp = tile
                c, n, f = blk
                s = stg[cnt[0] % 4]
                if f == "silu":
                    self.act(s[0:tp, 0:n], bk[0:tp, 0:n], AF.Silu)
                elif f == "sigmoid":
                    self.act(s[0:tp, 0:n], bk[0:tp, 0:n], AF.Sigmoid)
                else:
                    self.cp("dve" if cnt[0] % 2 == 0 else "act", s[0:tp, 0:n], bk[0:tp, 0:n])
                cnt[0] += 1
                if c < 3072:
                    if i < 16:
                        self.store(self.s_qkva[3 + c0:3 + c0 + tp, c:c + n], s[0:tp, 0:n], dram=[self.bqkva[i]])
                    else:
                        self.store(self.s_qkva_s[:, c:c + n], s[0:tp, 0:n], dram=[self.bqkva[i]])
                else:
                    self.store(self.s_proj[c0:c0 + tp, c:c + n], s[0:tp, 0:n], dram=[self.bproj[i]])

            self.gemm_res(es, XT, KC, blocks, lambda blk: self.w_in[l, :, blk[0]:blk[0] + blk[1]], evac)
            cs1 = self.sb(es, "cst1", [3, 3072])
            cs2 = self.sb(es, "cst2", [NS, 3, 3072])
            self.load(cs1[:], self.s_qkva[SEQ:SEQ + 3, :], dram=[self.bqkva[15]])
            self.out_store(self.o_dconv_p[l], cs1[:])
            self.load(cs2[:, 0:2, :], self.st_dconv[l, :, 1:3, :])
            self.load(cs2[:, 2, :], self.s_qkva_s[:, :], dram=[self.bqkva[16]], partial=True)
            self.out_store(self.o_dconv_s[l], cs2[:])
            kb.barrier()

    def out_store(self, dst_ap, src, q="sp"):
        return self.kb.dma(q, dst_ap, src.ap, tag=src.b, reads=[src.b])

    def gated_norm_store(self, i, tp, o, z_src_col, gain, nh, dv, slot, tmp):
        for _ in self.gated_norm_gen(i, tp, o, z_src_col, gain, nh, dv, slot, tmp):
            pass

    def gated_norm_gen(self, i, tp, o, z_src_col, gain, nh, dv, slot, tmp):
        sc, junk, z, br, brs = tmp
        c0 = self.TILES[i][1]
        self.load(z[0:tp, :], self.s_proj[c0:c0 + tp, z_src_col:z_src_col + 1024], dram=[self.bproj[i]])
        self.memset("dve", sc[0:tp, 0:nh], 0.0)
        yield
        for h in range(nh):
            self.act(junk[0:tp, 0:dv], o[0:tp, h * dv:(h + 1) * dv], AF.Square, accum=sc[0:tp, h:h + 1])
            if h % 2 == 1:
                yield
        self.act(sc[0:tp, 8:8 + nh], sc[0:tp, 0:nh], AF.Sqrt, bias=EPS, scale=1.0 / dv)
        self.recip(sc[0:tp, 8:8 + nh], sc[0:tp, 8:8 + nh])
        yield
        o3 = o[0:tp, :].r("p (h d) -> p h d", d=dv)
        self.tt("dve", o3, o3, sc[0:tp, 8:8 + nh].us(2).bc([tp, nh, dv]), ALU.mult)
        yield
        if gain is not None:
            self.tt("pool", o3, o3, gain[0:tp, :].us(1).bc([tp, nh, dv]), ALU.mult)
            yield
        self.tt("dve", br[0:tp, :], o[0:tp, :], z[0:tp, :], ALU.mult)
        yield
        bk = self.bank()
        pv = bk[:].cast(BF16)
        self.tr(bk, [(pv[:, k * 128:k * 128 + tp], br[0:tp, k * 128:(k + 1) * 128]) for k in range(8)], self.idb[0:tp, 0:tp])
        self.cp("act", brs[:, :, 0:tp], pv.r("p (k c) -> p k c", c=128)[:, :, 0:tp])
        dst = self.s_brT[i, :, slot * 1024:(slot + 1) * 1024].rearrange("p (k c) -> p k c", c=128)[:, :, 0:tp]
        self.store(dst, brs[:, :, 0:tp], dram=[self.bbrT[i]])
        yield

    def bcast_cols(self, out, x, ncol, tmpX):
        X3 = tmpX[0:NS, 0:NS * ncol].r("p (s c) -> p s c", c=ncol)
        self.tt("dve", X3, x.us(1).bc([NS, NS, ncol]), self.idf[0:NS, 0:NS].us(2).bc([NS, NS, ncol]), ALU.mult)
        bk = self.bank()
        self.mm(bk, [(bk[:, 0:NS * ncol], self.cst_t[0:NS, C_ONE:C_ONE + 128], tmpX[0:NS, 0:NS * ncol], True, True)])
        self.cp("act", out, bk[:, 0:NS * ncol])

    def mixA(self, l):
        kb = self.kb
        C = self.cst_t
        U, SL, ONE, MS, IDF = (C[:, C_U:C_U + 128], C[:, C_SL:C_SL + 128], C[:, C_ONE:C_ONE + 128],
                               C[:, C_MS:C_MS + 128], self.idf)
        with ExitStack() as es:
            cw = self.sb(es, "cw", [128, 4, 3072])
            PWA = 512
            xq = [self.sb(es, "xq%d" % i, [128, 4, PWA]) for i in range(2)]
            qkvs = [self.sb(es, "qkv%d" % i, [128, 3072]) for i in range(2)]
            tmpc = self.sb(es, "tmpc", [128, PWA])
            za = self.sb(es, "za", [128, 1024])
            sm = self.sb(es, "smA", [128, 16])
            par = self.sb(es, "parA", [128, 16])
            negA = self.sb(es, "negA", [128, 8])
            gab = self.sb(es, "gab", [128, 128])
            oalls = [self.sb(es, "oallA%d" % i, [128, 1024]) for i in range(2)]
            br = self.sb(es, "brA", [128, 1024], BF16)
            brs = self.sb(es, "brsA", [128, 8, 128], BF16)
            junk = self.sb(es, "junkA", [128, 256], BF16)
            scs = [self.sb(es, "scA%d" % i, [128, 64]) for i in range(2)]
            sc2s = [self.sb(es, "sc2A%d" % i, [128, 64]) for i in range(2)]
            sc3 = self.sb(es, "sc3A", [128, 16])
            self.load(cw[:].r("p s c -> p (s c)"), self.conv_a_w[l:l + 1].rearrange("a s c -> a (s c)").partition_broadcast(128))
            self.load(par[:, 0:8], self.a_log[l:l + 1, :].partition_broadcast(128))
            self.load(par[:, 8:16], self.dt_bias[l:l + 1, :].partition_broadcast(128), partial=True)
            self.load(gab[:], self.norm_a_g[l:l + 1, :].partition_broadcast(128))
            self.act(negA[:], par[:, 0:8], AF.Exp)
            self.ts("dve", negA[:], negA[:], -1.0, ALU.mult)
            nconv = [0]

            def front_gen(i, c0, tp):
                qkv, sc, sc2 = qkvs[i % 2], scs[i % 2], sc2s[i % 2]
                for pt in range(3072 // PWA):
                    pc = pt * PWA
                    x4 = xq[nconv[0] % 2]
                    nconv[0] += 1
                    for s in range(4):
                        if i < 16:
                            src = self.s_qkva[c0 + s:c0 + s + 128, pc:pc + PWA]
                            dr = [self.bqkva[max(i - 1, 0)], self.bqkva[i]]
                        elif s < 3:
                            src, dr = self.st_dconv[l, :, s, pc:pc + PWA], []
                        else:
                            src, dr = self.s_qkva_s[:, pc:pc + PWA], [self.bqkva[16]]
                        self.load(x4[0:tp, s, :], src, dram=dr, partial=(s > 0))
                    yield
                    acc = qkv[0:tp, pc:pc + PWA]
                    self.tt("pool", acc, x4[0:tp, 0, :], cw[0:tp, 0, pc:pc + PWA], ALU.mult)
                    yield
                    for s in range(1, 4):
                        self.tt("pool", tmpc[0:tp, :], x4[0:tp, s, :], cw[0:tp, s, pc:pc + PWA], ALU.mult)
                        self.tt("dve", acc, acc, tmpc[0:tp, :], ALU.add)
                        yield
                    if pt % 2 == 1:
                        big = qkv[0:tp, pc - PWA:pc + PWA]
                        self.act(big, big, AF.Silu)
                        yield
                self.memset("dve", sc2[0:tp, 0:16], 0.0)
                for h in range(16):
                    self.act(junk[0:tp, 0:128], qkv[0:tp, h * 128:(h + 1) * 128], AF.Square, accum=sc2[0:tp, h:h + 1])
                    if h % 2 == 1:
                        yield
                self.act(sc2[0:tp, 16:32], sc2[0:tp, 0:16], AF.Sqrt, bias=EPS)
                self.recip(sc2[0:tp, 16:32], sc2[0:tp, 16:32])
                self.ts("dve", sc2[0:tp, 16:24], sc2[0:tp, 16:24], 128.0 ** -0.5, ALU.mult)
                yield
                qk3 = qkv[0:tp, 0:2048].r("p (h d) -> p h d", d=128)
                self.tt("dve", qk3, qk3, sc2[0:tp, 16:32].us(2).bc([tp, 16, 128]), ALU.mult)
                yield
                self.load(sm[0:tp, :], self.s_proj[c0:c0 + tp, O_BETA:O_BETA + 16], dram=[self.bproj[i]])
                self.act(sc[0:tp, 0:8], sm[0:tp, 0:8], AF.Sigmoid)
                self.tt("dve", sc[0:tp, 8:16], sm[0:tp, 8:16], par[0:tp, 8:16], ALU.add)
                self.act(sc[0:tp, 8:16], sc[0:tp, 8:16], AF.Exp)
                self.act(sc[0:tp, 8:16], sc[0:tp, 8:16], AF.Ln, bias=1.0)
                self.tt("dve", sc[0:tp, 8:16], sc[0:tp, 8:16], negA[0:tp, :], ALU.mult)
                yield

            def exhaust(gen):
                for _ in gen:
                    pass

            def back_gen(i, tp):
                return self.gated_norm_gen(i, tp, oalls[i % 2], O_ZA, gab, 8, 128, 0, (sc3, junk, za, br, brs))

            with ExitStack() as e1:
                Ss = [self.sb(e1, "SA%d" % j, [128, 512]) for j in range(2)]
                EdT = self.sb(e1, "EdT", [128, 1024])
                EdTs = self.sb(e1, "EdTs", [128, 1024])
                Ug = EdTs
                names = ["kb_", "qin", "kout", "Rw", "Rv", "kT", "kbT", "qT", "qinT", "LT", "L", "M0", "M1", "MT0", "MT1", "PT"]
                tsets = [{n: self.sb(e1, "A%d_%s" % (j, n), [128, 512]) for n in names} for j in range(2)]
                for tt_ in tsets:
                    tt_.update({"Y": tt_["kb_"], "Bm": tt_["qin"], "X": tt_["LT"], "L16": tt_["M1"], "L16T": tt_["MT1"],
                                "wT": tt_["M0"], "uv": tt_["M1"], "qkT": tt_["MT0"], "u": tt_["MT1"]})
                for S_ in Ss:
                    self.memset("dve", S_[:], 0.0)
                r3 = lambda v: v.r("p (h d) -> p h d", d=128)
                hv = lambda v, h: v[:, h * 128:(h + 1) * 128]
                m4 = lambda col: C[:, col:col + 128].us(1).bc([128, 4, 128])

                def hg_chain(hg, t, qkv, sc, oall):
                    h0 = hg * 4
                    qh = qkv[:, h0 * 128:h0 * 128 + 512]
                    kh = qkv[:, 1024 + h0 * 128:1024 + h0 * 128 + 512]
                    vh = qkv[:, 2048 + h0 * 128:2048 + h0 * 128 + 512]
                    bc4 = lambda col: sc[:, col + h0:col + h0 + 4].us(2).bc([128, 4, 128])
                    self.tt("dve", r3(t["kb_"][:]), r3(kh), bc4(0), ALU.mult)
                    self.tt("pool", r3(t["qin"][:]), r3(qh), bc4(24), ALU.mult)
                    self.tt("dve", r3(t["kout"][:]), r3(kh), bc4(40), ALU.mult)
                    self.tt("pool", r3(t["Rw"][:]), r3(kh), bc4(48), ALU.mult)
                    self.tt("dve", r3(t["Rv"][:]), r3(vh), bc4(0), ALU.mult)
                    yield
                    for (dst, src) in (("kT", kh), ("kbT", t["kb_"][:]), ("qT", qh), ("qinT", t["qin"][:])):
                        bk = self.bank()
                        self.tr(bk, [(hv(bk, h), hv(src, h)) for h in range(4)], IDF)
                        self.cp("act", t[dst][:], bk[:])
                        yield
                    bk = self.bank()
                    self.mm(bk, [(hv(bk, h), hv(t["kT"], h), hv(t["kbT"], h), True, True) for h in range(4)])
                    self.tt("dve", t["LT"][:], bk[:], EdTs[:, h0 * 128:h0 * 128 + 512], ALU.mult)
                    yield
                    bk = self.bank()
                    self.tr(bk, [(hv(bk, h), hv(t["LT"], h)) for h in range(4)], IDF)
                    self.cp("act", t["L"][:], bk[:])
                    self.tt("dve", r3(t["L16T"][:]), r3(t["LT"][:]), m4(C_BD16), ALU.mult)
                    yield
                    self.tt("pool", r3(t["L16"][:]), r3(t["L"][:]), m4(C_BD16), ALU.mult)
                    self.stt("dve", r3(t["PT"][:]), r3(t["L16T"][:]), -1.0, IDF.us(1).bc([128, 4, 128]), ALU.mult, ALU.add)
                    yield
                    M, MT = t["L16"], t["L16T"]
                    for r in range(3):
                        M2, M2T = (t["M0"], t["MT0"]) if r % 2 == 0 else (t["M1"], t["MT1"])
                        bkM = self.bank()
                        self.mm(bkM, [(hv(bkM, h), hv(MT, h), hv(M, h), True, True) for h in range(4)])
                        if r < 2:
                            bkT = self.bank()
                            self.mm(bkT, [(hv(bkT, h), hv(M, h), hv(MT, h), True, True) for h in range(4)])
                        self.cp("act", M2[:], bkM[:])
                        if r < 2:
                            self.cp("dve", M2T[:], bkT[:])
                        yield
                        bkP = self.bank()
                        self.mm(bkP, [(hv(bkP, h), hv(M2, h), hv(t["PT"], h), True, True) for h in range(4)])
                        self.tt("dve", t["PT"][:], t["PT"][:], bkP[:], ALU.add)
                        M, MT = M2, M2T
                        yield
                    for lv, mcol in enumerate((C_ML32, C_ML64, C_ML128)):
                        bk = self.bank()
                        self.tr(bk, [(hv(bk, h), hv(t["PT"], h)) for h in range(4)], IDF)
                        self.cp("act", t["X"][:], bk[:])
                        self.tt("pool", r3(t["Bm"][:]), r3(t["L"][:]), m4(mcol), ALU.mult)
                        yield
                        bk = self.bank()
                        self.mm(bk, [(hv(bk, h), hv(t["Bm"], h), hv(t["PT"], h), True, True) for h in range(4)])
                        self.cp("act", t["Y"][:], bk[:])
                        yield
                        bk = self.bank()
                        self.mm(bk, [(hv(bk, h), hv(t["X"], h), hv(t["Y"], h), True, True) for h in range(4)])
                        self.tt("dve", t["PT"][:], t["PT"][:], bk[:], ALU.subtract)
                        yield
                    bk = self.bank()
                    self.mm(bk, [(hv(bk, h), hv(t["Rw"], h), hv(t["PT"], h), True, True) for h in range(4)])
                    self.cp("act", t["wT"][:], bk[:])
                    yield
                    bk = self.bank()
                    self.mm(bk, [(hv(bk, h), hv(t["PT"], h), hv(t["Rv"], h), True, True) for h in range(4)])
                    self.cp("act", t["uv"][:], bk[:])
                    yield
                    bk = self.bank()
                    self.mm(bk, [(hv(bk, h), hv(t["kT"], h), hv(t["qT"], h), True, True) for h in range(4)])
                    self.tt("dve", t["qkT"][:], bk[:], EdT[:, h0 * 128:h0 * 128 + 512], ALU.mult)
                    yield
                    Sh = Ss[hg]
                    bk = self.bank()
                    self.mm(bk, [(hv(bk, h), hv(t["wT"], h), hv(Sh, h), True, True) for h in range(4)])
                    self.tt("dve", t["u"][:], t["uv"][:], bk[:], ALU.subtract)
                    yield
                    bk = self.bank()
                    items = []
                    for h in range(4):
                        items.append((hv(bk, h), hv(t["qinT"], h), hv(Sh, h), True, False))
                        items.append((hv(bk, h), hv(t["qkT"], h), hv(t["u"], h), False, True))
                    self.mm(bk, items)
                    self.cp("act", oall[:, h0 * 128:h0 * 128 + 512], bk[:])
                    yield
                    bk = self.bank()
                    self.mm(bk, [(hv(bk, h), hv(t["kout"], h), hv(t["u"], h), True, True) for h in range(4)])
                    for h in range(4):
                        self.stt("dve", hv(Sh, h), hv(Sh, h), sc[:, 32 + h0 + h:33 + h0 + h], hv(bk, h), ALU.mult, ALU.add)
                    yield

                exhaust(front_gen(*self.TILES[0]))
                pending = None
                for (i, c0, tp) in self.TILES[:16]:
                    qkv, sc, oall = qkvs[i % 2], scs[i % 2], oalls[i % 2]
                    g = sc[:, 8:16]
                    bk = self.bank()
                    self.mm(bk, [(bk[:, 0:8], U, g, True, True), (bk[:, 8:16], ONE, g, True, True)])
                    self.cp("act", sc[:, 16:24], bk[:, 0:8])
                    self.act(sc[:, 24:32], bk[:, 0:8], AF.Exp)
                    self.act(sc[:, 32:40], bk[:, 8:16], AF.Exp)
                    self.tt("dve", sc[:, 40:48], bk[:, 8:16], sc[:, 16:24], ALU.subtract)
                    self.act(sc[:, 40:48], sc[:, 40:48], AF.Exp)
                    self.tt("dve", sc[:, 48:56], sc[:, 0:8], sc[:, 24:32], ALU.mult)
                    self.tt("dve", r3(Ug[:]), U.us(1).bc([128, 8, 128]), g.us(2).bc([128, 8, 128]), ALU.mult)
                    bks = [self.bank(), self.bank()]
                    for half in range(2):
                        self.mm(bks[half], [(bks[half][:], SL, Ug[:, half * 512:(half + 1) * 512], True, True)])
                    for half in range(2):
                        self.act(EdT[:, half * 512:(half + 1) * 512], bks[half][:], AF.Exp)
                    self.tt("dve", r3(EdTs[:]), r3(EdT[:]), MS.us(1).bc([128, 8, 128]), ALU.mult)
                    self.tt("pool", r3(EdT[:]), r3(EdT[:]), U.us(1).bc([128, 8, 128]), ALU.mult)
                    gens = [hg_chain(0, tsets[0], qkv, sc, oall), hg_chain(1, tsets[1], qkv, sc, oall)]
                    if pending is not None:
                        gens.append(pending)
                    if i + 1 < 16:
                        gens.append(front_gen(*self.TILES[i + 1]))
                    while gens:
                        for gch in list(gens):
                            try:
                                next(gch)
                            except StopIteration:
                                gens.remove(gch)
                    pending = back_gen(i, tp)
                exhaust(pending)
                for j in range(2):
                    self.out_store(self.o_delta_p[l, 4 * j:4 * j + 4].rearrange("h k v -> k h v"), r3(Ss[j][:]))
                kb.barrier()
            with ExitStack() as e2:
                i, c0, tp = self.TILES[16]
                kTs = self.sb(e2, "kTs", [128, 8, NS])
                qTs = self.sb(e2, "qTs", [128, 8, NS])
                kTd = self.sb(e2, "kTd", [128, 8 * NS * NS])
                qTd = self.sb(e2, "qTd", [128, 8 * NS * NS])
                Sst = [self.sb(e2, "Sst%d" % j, [128, 2, 8, 128]) for j in range(2)]
                Snw = [self.sb(e2, "Snw%d" % j, [128, 2, 8, 128]) for j in range(2)]
                kd = self.sb(e2, "kd", [NS, 8, 2, 128])
                u = self.sb(e2, "uS", [NS, 1024])
                tX = self.sb(e2, "tX", [NS, 128])
                egbc = self.sb(e2, "egbc", [128, 128])
                qkv, sc, oall = qkvs[i % 2], scs[i % 2], oalls[i % 2]
                exhaust(front_gen(i, c0, tp))
                self.act(sc[0:tp, 16:24], sc[0:tp, 8:16], AF.Exp)
                self.bcast_cols(egbc[:], sc[0:NS, 16:24], 8, tX)
                for (dst, dd, off) in ((kTs, kTd, 1024), (qTs, qTd, 0)):
                    bk = self.bank()
                    self.tr(bk, [(bk[:, h * NS:(h + 1) * NS], qkv[0:NS, off + h * 128:off + (h + 1) * 128]) for h in range(8)],
                            IDF[0:NS, 0:NS])
                    self.cp("act", dst[:].r("p h s -> p (h s)"), bk[:, 0:8 * NS])
                    d4 = dd[:].r("p (h a b) -> p h a b", a=NS, b=NS)
                    self.tt("dve", d4, dst[:].us(2).bc([128, 8, NS, NS]),
                            C[:, C_I16:C_I16 + 256].r("p (a b) -> p a b", b=NS).us(1).bc([128, 8, NS, NS]), ALU.mult)
                kTd4 = kTd[:].r("p (h a b) -> p h a b", a=NS, b=NS)
                qTd4 = qTd[:].r("p (h a b) -> p h a b", a=NS, b=NS)
                self.bank_pool = [0, 1, 2, 3]
                bR = [self.ps[4], self.ps[5]]
                bO = [self.ps[6], self.ps[7]]
                for cch in range(8):
                    St = Sst[cch % 2]
                    self.load(St[:], self.st_delta[l, 2 * cch:2 * cch + 2].rearrange("s h k v -> k s h v"))
                    for half in range(2):
                        items = []
                        for sl in range(2):
                            s = 2 * cch + sl
                            for h in range(half * 4, half * 4 + 4):
                                items.append((bR[half][0:NS, (h % 4) * 128:(h % 4 + 1) * 128], kTd4[:, h, s, :], St[:, sl, h, :],
                                              s == 0 and h % 4 == 0, s == NS - 1))
                        self.mm(bR[half], items)
                u3 = u[:].r("p (h d) -> p h d", d=128)
                for half in range(2):
                    uh = u[:, half * 512:(half + 1) * 512].r("p (h d) -> p h d", d=128)
                    self.tt("dve", uh, bR[half][0:NS, :].r("p (h d) -> p h d", d=128),
                            sc[0:NS, 16 + half * 4:20 + half * 4].us(2).bc([NS, 4, 128]), ALU.mult)
                self.tt("dve", u[:], qkv[0:NS, 2048:3072], u[:], ALU.subtract)
                self.tt("dve", u3, u3, sc[0:NS, 0:8].us(2).bc([NS, 8, 128]), ALU.mult)
                k3 = qkv[0:NS, 1024:2048].r("p (h d) -> p h d", d=128)
                for cch in range(8):
                    St, Sn = Sst[cch % 2], Snw[cch % 2]
                    self.load(St[:], self.st_delta[l, 2 * cch:2 * cch + 2].rearrange("s h k v -> k s h v"))
                    for sl in range(2):
                        s = 2 * cch + sl
                        self.tt("dve", kd[:, :, sl, :], k3, IDF[0:NS, s:s + 1].us(2).bc([NS, 8, 128]), ALU.mult)
                    for sl in range(2):
                        s = 2 * cch + sl
                        for half in range(2):
                            bk = self.bank()
                            self.mm(bk, [(bk[:, (h % 4) * 128:(h % 4 + 1) * 128], kd[:, h, sl, :], u[:, h * 128:(h + 1) * 128], True, True)
                                         for h in range(half * 4, half * 4 + 4)])
                            for h in range(half * 4, half * 4 + 4):
                                self.stt("dve", Sn[:, sl, h, :], St[:, sl, h, :], egbc[:, s * 8 + h:s * 8 + h + 1],
                                         bk[:, (h % 4) * 128:(h % 4 + 1) * 128], ALU.mult, ALU.add)
                    for half in range(2):
                        items = []
                        for sl in range(2):
                            s = 2 * cch + sl
                            for h in range(half * 4, half * 4 + 4):
                                items.append((bO[half][0:NS, (h % 4) * 128:(h % 4 + 1) * 128], qTd4[:, h, s, :], Sn[:, sl, h, :],
                                              s == 0 and h % 4 == 0, s == NS - 1))
                        self.mm(bO[half], items)
                    self.out_store(self.o_delta_s[l, 2 * cch:2 * cch + 2].rearrange("s h k v -> k s h v"), Sn[:])
                for half in range(2):
                    self.cp("act", oall[0:NS, half * 512:(half + 1) * 512], bO[half][0:NS, :])
                self.bank_pool = list(range(8))
                exhaust(back_gen(i, tp))
                kb.barrier()

    def sample_step_kv(self, e2, l, st_in, st_out, q_tm, k_tm, v_tm, oall, decay):
        IDF, C = self.idf, self.cst_t
        qTs = self.sb(e2, "qTs2", [128, 4, NS])
        qTd = self.sb(e2, "qTd2", [128, 4 * NS * NS])
        Sst = [self.sb(e2, "Sst2_%d" % j, [128, 2, 4, 256]) for j in range(2)]
        Snw = [self.sb(e2, "Snw2_%d" % j, [128, 2, 4, 256]) for j in range(2)]
        kd = self.sb(e2, "kd2", [NS, 4, 2, 128])
        bk = self.bank()
        self.tr(bk, [(bk[:, h * NS:(h + 1) * NS], q_tm[:, h * 128:(h + 1) * 128]) for h in range(4)], IDF[0:NS, 0:NS])
        self.cp("act", qTs[:].r("p h s -> p (h s)"), bk[:, 0:4 * NS])
        qTd4 = qTd[:].r("p (h a b) -> p h a b", a=NS, b=NS)
        self.tt("dve", qTd4, qTs[:].us(2).bc([128, 4, NS, NS]),
                C[:, C_I16:C_I16 + 256].r("p (a b) -> p a b", b=NS).us(1).bc([128, 4, NS, NS]), ALU.mult)
        k3 = k_tm.r("p (h d) -> p h d", d=128)
        self.bank_pool = [0, 1, 2, 3, 4, 5]
        bO = [self.ps[6], self.ps[7]]
        for cch in range(8):
            St, Sn = Sst[cch % 2], Snw[cch % 2]
            self.load(St[:], st_in[l, 2 * cch:2 * cch + 2].rearrange("s h k v -> k s h v"))
            for sl in range(2):
                s = 2 * cch + sl
                self.tt("dve", kd[:, :, sl, :], k3, IDF[0:NS, s:s + 1].us(2).bc([NS, 4, 128]), ALU.mult)
            for sl in range(2):
                s = 2 * cch + sl
                for half in range(2):
                    bk = self.bank()
                    self.mm(bk, [(bk[:, (h % 2) * 256:(h % 2 + 1) * 256], kd[:, h, sl, :], v_tm[:, h * 256:(h + 1) * 256], True, True)
                                 for h in range(half * 2, half * 2 + 2)])
                    for h in range(half * 2, half * 2 + 2):
                        self.stt("dve", Sn[:, sl, h, :], St[:, sl, h, :], decay(h, s), bk[:, (h % 2) * 256:(h % 2 + 1) * 256],
                                 ALU.mult, ALU.add)
            for half in range(2):
                items = []
                for sl in range(2):
                    s = 2 * cch + sl
                    for h in range(half * 2, half * 2 + 2):
                        items.append((bO[half][0:NS, (h % 2) * 256:(h % 2 + 1) * 256], qTd4[:, h, s, :], Sn[:, sl, h, :],
                                      s == 0 and h % 2 == 0, s == NS - 1))
                self.mm(bO[half], items)
            self.out_store(st_out[l, 2 * cch:2 * cch + 2].rearrange("s h k v -> k s h v"), Sn[:])
        for half in range(2):
            self.cp("act", oall[0:NS, half * 512:(half + 1) * 512], bO[half][0:NS, :])
        self.bank_pool = list(range(8))

    def mixB(self, l):
        kb = self.kb
        C = self.cst_t
        IDF = self.idf
        DTc, GIN, GOUT = C[:, C_DT:C_DT + 512], C[:, C_GIN:C_GIN + 512], C[:, C_GOUT:C_GOUT + 4]
        gam = [float(x) for x in self.gam]
        with ExitStack() as es:
            qk = self.sb(es, "qkB", [128, 1024])
            v = self.sb(es, "vB", [128, 1024])
            rp = self.sb(es, "rpB", [128, 128])
            tt_ = [self.sb(es, "ropeT%d" % j, [128, 512]) for j in range(4)]
            qkr = self.sb(es, "qkrB", [128, 1024])
            oall = self.sb(es, "oallB", [128, 1024])
            z = self.sb(es, "zB", [128, 1024])
            br = self.sb(es, "brB", [128, 1024], BF16)
            brs = self.sb(es, "brsB", [128, 8, 128], BF16)
            junk = self.sb(es, "junkB", [128, 256], BF16)
            sc3 = self.sb(es, "sc3B", [128, 16])

            def front(i, c0, tp):
                self.load(qk[0:tp, :], self.s_proj[c0:c0 + tp, O_QB:O_QB + 1024], dram=[self.bproj[i]])
                self.load(v[0:tp, :], self.s_proj[c0:c0 + tp, O_VB:O_VB + 1024], dram=[self.bproj[i]])
                self.load(rp[0:tp, :], self.rope_p[c0:c0 + tp, :] if i < 16 else self.rope_s[:, :])
                x3 = qk[0:tp, :].r("p (h d) -> p h d", d=128)
                o3 = qkr[0:tp, :].r("p (h d) -> p h d", d=128)
                x1, x2 = x3[:, :, 0:64], x3[:, :, 64:128]
                cos = rp[0:tp, 0:64].us(1).bc([tp, 8, 64])
                sin = rp[0:tp, 64:128].us(1).bc([tp, 8, 64])
                t = [a[0:tp, :].r("p (h d) -> p h d", d=64) for a in tt_]
                self.tt("dve", t[0], x1, cos, ALU.mult)
                self.tt("pool", t[1], x2, sin, ALU.mult)
                self.tt("dve", o3[:, :, 0:64], t[0], t[1], ALU.subtract)
                self.tt("pool", t[2], x2, cos, ALU.mult)
                self.tt("dve", t[3], x1, sin, ALU.mult)
                self.tt("pool", o3[:, :, 64:128], t[2], t[3], ALU.add)
                self.ts("dve", qkr[0:tp, 512:1024], qkr[0:tp, 512:1024], 128.0 ** -0.5, ALU.mult)

            def back(i, tp):
                self.gated_norm_store(i, tp, oall, O_ZB, None, 4, 256, 1, (sc3, junk, z, br, brs))

            hv = lambda x, h: x[:, h * 128:(h + 1) * 128]
            hw = lambda x, h: x[:, h * 256:(h + 1) * 256]
            with ExitStack() as e1:
                S = self.sb(e1, "SB", [128, 1024])
                qT = self.sb(e1, "qTB", [128, 512])
                kT = self.sb(e1, "kTB", [128, 512])
                PT = self.sb(e1, "PTB", [128, 512])
                qinT = self.sb(e1, "qinTB", [128, 512])
                kout = self.sb(e1, "koutB", [128, 512])
                self.memset("dve", S[:], 0.0)
                for (i, c0, tp) in self.TILES[:16]:
                    front(i, c0, tp)
                    for (dst, off) in ((qT, 0), (kT, 512)):
                        bk = self.bank()
                        self.tr(bk, [(hv(bk, h), qkr[:, off + h * 128:off + (h + 1) * 128]) for h in range(4)], IDF)
                        self.cp("act", dst[:], bk[:])
                    bk = self.bank()
                    self.mm(bk, [(hv(bk, h), hv(kT, h), hv(qT, h), True, True) for h in range(4)])
                    self.tt("dve", PT[:], bk[:], DTc, ALU.mult)
                    self.tt("pool", qinT[:], qT[:], GIN, ALU.mult)
                    self.tt("dve", kout[:].r("p (h d) -> p h d", d=128), qkr[:, 512:1024].r("p (h d) -> p h d", d=128),
                            GOUT.us(2).bc([128, 4, 128]), ALU.mult)
                    for half in range(2):
                        bk = self.bank()
                        items = []
                        for h in range(half * 2, half * 2 + 2):
                            items.append((hw(bk, h % 2), hv(PT, h), hw(v, h), True, False))
                            items.append((hw(bk, h % 2), hv(qinT, h), hw(S, h), False, True))
                        self.mm(bk, items)
                        self.cp("act", oall[:, half * 512:(half + 1) * 512], bk[:])
                    for half in range(2):
                        bk = self.bank()
                        self.mm(bk, [(hw(bk, h % 2), hv(kout, h), hw(v, h), True, True) for h in range(half * 2, half * 2 + 2)])
                        for h in range(half * 2, half * 2 + 2):
                            self.stt("dve", hw(S, h), hw(S, h), gam[h] ** 128, hw(bk, h % 2), ALU.mult, ALU.add)
                    back(i, tp)
                self.out_store(self.o_ret_p[l].rearrange("h k v -> k h v"), S[:].r("p (h v) -> p h v", v=256))
                kb.barrier()
            with ExitStack() as e2:
                i, c0, tp = self.TILES[16]
                front(i, c0, tp)
                self.sample_step_kv(e2, l, self.st_ret, self.o_ret_s, qkr[0:NS, 0:512], qkr[0:NS, 512:1024], v[0:NS, :], oall,
                                    lambda h, s: gam[h])
                back(i, tp)
                kb.barrier()

    def mixC(self, l):
        kb = self.kb
        C = self.cst_t
        IDF = self.idf
        U, ONE = C[:, C_U:C_U + 128], C[:, C_ONE:C_ONE + 128]
        with ExitStack() as es:
            qk = self.sb(es, "qkC", [128, 1024])
            v = self.sb(es, "vC", [128, 1024])
            lr = self.sb(es, "lrC", [128, 16])
            w2 = self.sb(es, "w2C", [32, 512])
            lrT = self.sb(es, "lrT", [32, 128])
            g = self.sb(es, "gC", [128, 512])
            e1_ = self.sb(es, "e1C", [128, 512])
            qt = self.sb(es, "qtC", [128, 512])
            gcb = self.sb(es, "gcb", [128, 256])
            oall = self.sb(es, "oallC", [128, 1024])
            z = self.sb(es, "zC", [128, 1024])
            br = self.sb(es, "brC", [128, 1024], BF16)
            brs = self.sb(es, "brsC", [128, 8, 128], BF16)
            junk = self.sb(es, "junkC", [128, 256], BF16)
            sc3 = self.sb(es, "sc3C", [128, 16])
            self.load(w2[0:16, :], self.w_lr2[l])
            self.load(w2[16:17, :], self.b_lr2[l:l + 1, :], partial=True)
            self.load(gcb[:], self.norm_c_g[l:l + 1, :].partition_broadcast(128))
            self.memset("dve", lrT[:], 1.0)

            def front(i, c0, tp):
                self.load(qk[0:tp, :], self.s_proj[c0:c0 + tp, O_QC:O_QC + 1024], dram=[self.bproj[i]])
                self.load(v[0:tp, :], self.s_proj[c0:c0 + tp, O_VC:O_VC + 1024], dram=[self.bproj[i]])
                self.load(lr[0:tp, :], self.s_proj[c0:c0 + tp, O_LR:O_LR + 16], dram=[self.bproj[i]])
                bk = self.bank()
                self.tr(bk, [(bk[0:16, 0:tp], lr[0:tp, 0:16])], IDF[0:tp, 0:tp])
                self.cp("act", lrT[0:16, 0:tp], bk[0:16, 0:tp])
                bk = self.bank()
                self.mm(bk, [(bk[0:tp, :], lrT[0:17, 0:tp], w2[0:17, :], True, True)])
                self.act(g[0:tp, :], bk[0:tp, :], AF.Exp, scale=-1.0)
                self.act(g[0:tp, :], g[0:tp, :], AF.Ln, bias=1.0)
                self.ts("dve", g[0:tp, :], g[0:tp, :], -1.0 / 16.0, ALU.mult)

            def back(i, tp):
                self.gated_norm_store(i, tp, oall, O_ZC, gcb, 4, 256, 2, (sc3, junk, z, br, brs))

            hv = lambda x, h: x[:, h * 128:(h + 1) * 128]
            hw = lambda x, h: x[:, h * 256:(h + 1) * 256]
            with ExitStack() as e1:
                S = self.sb(e1, "SC", [128, 1024])
                bsb = self.sb(e1, "bsbC", [128, 512])
                kt = self.sb(e1, "ktC", [128, 512])
                kout = self.sb(e1, "koutC", [128, 512])
                qtT = self.sb(e1, "qtTC", [128, 512])
                ktT = self.sb(e1, "ktTC", [128, 512])
                PT = self.sb(e1, "PTC", [128, 512])
                cdT = self.sb(e1, "cdTC", [128, 4])
                self.memset("dve", S[:], 0.0)
                for (i, c0, tp) in self.TILES[:16]:
                    front(i, c0, tp)
                    bk1 = self.bank()
                    self.mm(bk1, [(bk1[:], U, g[:], True, True)])
                    bk2 = self.bank()
                    self.mm(bk2, [(bk2[:], ONE, g[:], True, True)])
                    self.act(e1_[:], bk1[:], AF.Exp)
                    self.stt("dve", qt[:], qk[:, 0:512], 128.0 ** -0.5, e1_[:], ALU.mult, ALU.mult)
                    self.act(e1_[:], bk1[:], AF.Exp, scale=-1.0)
                    self.tt("dve", kt[:], qk[:, 512:1024], e1_[:], ALU.mult)
                    self.cp("act", bsb[:], bk1[:])
                    self.tt("dve", e1_[:], bk2[:], bsb[:], ALU.subtract)
                    self.act(e1_[:], e1_[:], AF.Exp)
                    self.tt("dve", kout[:], qk[:, 512:1024], e1_[:], ALU.mult)
                    bk3 = self.bank()
                    self.mm(bk3, [(bk3[:, h:h + 1], hv(g, h), ONE[:, 0:1], h == 0, True) for h in range(4)])
                    self.act(cdT[:], bk3[:, 0:4], AF.Exp)
                    for (dst, src) in ((qtT, qt), (ktT, kt)):
                        bk = self.bank()
                        self.tr(bk, [(hv(bk, h), hv(src, h)) for h in range(4)], IDF)
                        self.cp("act", dst[:], bk[:])
                    bk = self.bank()
                    self.mm(bk, [(hv(bk, h), hv(ktT, h), hv(qtT, h), True, True) for h in range(4)])
                    self.tt("dve", PT[:].r("p (h d) -> p h d", d=128), bk[:].r("p (h d) -> p h d", d=128),
                            U.us(1).bc([128, 4, 128]), ALU.mult)
                    for half in range(2):
                        bk = self.bank()
                        items = []
                        for h in range(half * 2, half * 2 + 2):
                            items.append((hw(bk, h % 2), hv(PT, h), hw(v, h), True, False))
                            items.append((hw(bk, h % 2), hv(qtT, h), hw(S, h), False, True))
                        self.mm(bk, items)
                        self.cp("act", oall[:, half * 512:(half + 1) * 512], bk[:])
                    for half in range(2):
                        bk = self.bank()
                        self.mm(bk, [(hw(bk, h % 2), hv(kout, h), hw(v, h), True, True) for h in range(half * 2, half * 2 + 2)])
                        for h in range(half * 2, half * 2 + 2):
                            self.stt("dve", hw(S, h), hw(S, h), cdT[:, h:h + 1], hw(bk, h % 2), ALU.mult, ALU.add)
                    back(i, tp)
                self.out_store(self.o_gla_p[l].rearrange("h k v -> k h v"), S[:].r("p (h v) -> p h v", v=256))
                kb.barrier()
            with ExitStack() as e2:
                i, c0, tp = self.TILES[16]
                egT = self.sb(e2, "egT", [128, 4 * NS])
                front(i, c0, tp)
                self.act(e1_[0:NS, :], g[0:NS, :], AF.Exp)
                bk = self.bank()
                self.tr(bk, [(bk[:, h * NS:(h + 1) * NS], e1_[0:NS, h * 128:(h + 1) * 128]) for h in range(4)], IDF[0:NS, 0:NS])
                self.cp("act", egT[:], bk[:, 0:4 * NS])
                self.ts("dve", qt[0:NS, :], qk[0:NS, 0:512], 128.0 ** -0.5, ALU.mult)
                self.sample_step_kv(e2, l, self.st_gla, self.o_gla_s, qt[0:NS, :], qk[0:NS, 512:1024], v[0:NS, :], oall,
                                    lambda h, s: egT[:, h * NS + s:h * NS + s + 1])
                back(i, tp)
                kb.barrier()

    def x_update_evac(self, es, l, first, gcol, tag):
        gP = self.sb(es, "gP" + tag, [128, D])
        gS = self.sb(es, "gS" + tag, [NS, D])
        xo = [self.sb(es, "xo%s%d" % (tag, j), [128, 512]) for j in range(4)]
        tmp = [self.sb(es, "xt%s%d" % (tag, j), [128, 512]) for j in range(2)]
        self.load(gP[:], self.s_mod[l, NS:NS + 1, gcol:gcol + D].partition_broadcast(128), dram=[self.bmod[l]])
        self.load(gS[:], self.s_mod[l, 0:NS, gcol:gcol + D], dram=[self.bmod[l]])
        slot = {}
        cnt = [0]

        def pre(tile, blk):
            i, c0, tp = tile
            c, n = blk[0], blk[1]
            x = xo[cnt[0] % 4]
            t = tmp[cnt[0] % 2]
            cnt[0] += 1
            slot[(i, c)] = (x, t)
            if first and l == 0:
                src, dr = (self.xp[c0:c0 + tp, c:c + n] if i < 16 else self.xs[:, c:c + n]), []
            else:
                src, dr = self.s_x[c0:c0 + tp, c:c + n], [self.bx[i]]
            self.load(x[0:tp, 0:n], src, dram=dr)

        def evac(tile, blk, bk):
            i, c0, tp = tile
            c, n = blk[0], blk[1]
            x, t = slot.pop((i, c))
            gt = gP if i < 16 else gS
            self.tt("dve", t[0:tp, 0:n], bk[0:tp, 0:n], gt[0:tp, c:c + n], ALU.mult)
            self.tt("pool", x[0:tp, 0:n], x[0:tp, 0:n], t[0:tp, 0:n], ALU.add)
            self.store(self.s_x[c0:c0 + tp, c:c + n], x[0:tp, 0:n], dram=[self.bx[i]])
        return pre, evac

    def g23(self, l):
        kb = self.kb
        with ExitStack() as es:
            XT = self.sb(es, "XTm", [128, KC, NTOK], BF16)
            with ExitStack() as e:
                wsl = [self.sb(e, "g2w%d" % j, [128, 24, 512], BF16) for j in range(2)]
                bt = [self.sb(e, "g2b%d" % j, [128, 24, 128], BF16) for j in range(2)]
                gt = [self.sb(e, "g2g%d" % j, [128, 3, 512]) for j in range(2)]
                m = [self.sb(e, "g2m%d" % j, [128, 512]) for j in range(2)]
                mb = [self.sb(e, "g2mb%d" % j, [128, 512], BF16) for j in range(2)]
                tmp = [self.sb(e, "g2t%d" % j, [128, 512]) for j in range(2)]
                its = [(j, t) for j in range(4) for t in self.TILES]

                def loads(n):
                    j, (i, c0, tp) = its[n]
                    b, gg = bt[n % 2], gt[n % 2]
                    self.load(b[:, :, 0:tp], self.s_brT[i].rearrange("p (k c) -> p k c", c=128)[:, :, 0:tp], dram=[self.bbrT[i]])
                    self.load(gg[0:tp, :, :],
                              self.s_proj[c0:c0 + tp, O_MG:O_MG + 3 * D].rearrange("t (n c) -> t n c", n=3)[:, :, j * 512:(j + 1) * 512],
                              dram=[self.bproj[i]])

                loads(0)
                for n, (j, (i, c0, tp)) in enumerate(its):
                    w = wsl[j % 2]
                    if i == 0:
                        self.load(w[:], self.w_branch[l, :, j * 512:(j + 1) * 512].rearrange("(kc p) n -> p kc n", p=128), q="pool")
                    if n + 1 < len(its):
                        loads(n + 1)
                    b, gg, mm_, mbb = bt[n % 2], gt[n % 2], m[n % 2], mb[n % 2]
                    bks = []
                    for nb in range(3):
                        bk = self.bank()
                        self.mm(bk, [(bk[0:tp, :], b[:, 8 * nb + k, 0:tp], w[:, 8 * nb + k, :], k == 0, k == 7) for k in range(8)])
                        bks.append(bk)
                    self.tt("dve", mm_[0:tp, :], bks[0][0:tp, :], gg[0:tp, 0, :], ALU.mult)
                    self.tt("dve", tmp[0][0:tp, :], bks[1][0:tp, :], gg[0:tp, 1, :], ALU.mult)
                    self.tt("pool", mm_[0:tp, :], mm_[0:tp, :], tmp[0][0:tp, :], ALU.add)
                    self.tt("dve", tmp[1][0:tp, :], bks[2][0:tp, :], gg[0:tp, 2, :], ALU.mult)
                    self.tt("pool", mbb[0:tp, :], mm_[0:tp, :], tmp[1][0:tp, :], ALU.add)
                    bk = self.bank()
                    pv = bk[:].cast(BF16)
                    self.tr(bk, [(pv[:, k * 128:k * 128 + tp], mbb[0:tp, k * 128:(k + 1) * 128]) for k in range(4)],
                            self.idb[0:tp, 0:tp])
                    self.cp("act", XT[:, 4 * j:4 * j + 4, c0:c0 + tp], pv[:, 0:512].r("p (k c) -> p k c", c=128)[:, :, 0:tp])
                kb.barrier()
            with ExitStack() as e:
                pre, evac = self.x_update_evac(e, l, True, 2 * D, "3")
                self.gemm_res(e, XT, KC, [(j * 512, 512) for j in range(4)],
                              lambda blk: self.w_out[l, :, blk[0]:blk[0] + blk[1]], evac, pre=pre)
                kb.barrier()

    def g4(self, l):
        kb = self.kb
        with ExitStack() as es:
            XT = self.sb(es, "XTf", [128, KC, NTOK], BF16)
            self.norm_phase(l, 2, XT)
            with ExitStack() as e:
                stg = [self.sb(e, "g4s%d" % j, [128, 512]) for j in range(4)]
                cnt = [0]

                def evac(tile, blk, bk):
                    i, c0, tp = tile
                    c, n = blk[0], blk[1]
                    s = stg[cnt[0] % 4]
                    self.cp("dve" if cnt[0] % 2 == 0 else "act", s[0:tp, 0:n], bk[0:tp, 0:n])
                    cnt[0] += 1
                    if c < DFF:
                        if i < 16:
                            self.store(self.s_ua[2 + c0:2 + c0 + tp, c:c + n], s[0:tp, 0:n], dram=[self.bua[i]])
                        else:
                            self.store(self.s_ua_s[:, c:c + n], s[0:tp, 0:n], dram=[self.bua[i]])
                    else:
                        self.store(self.s_ub[c0:c0 + tp, c - DFF:c - DFF + n], s[0:tp, 0:n], dram=[self.bub[i]])

                self.gemm_res(e, XT, KC, [(j * 512, 512) for j in range(22)],
                              lambda blk: self.w_ffn_in[l, :, blk[0]:blk[0] + blk[1]], evac)
                cs1 = self.sb(e, "fst1", [2, DFF])
                cs2 = self.sb(e, "fst2", [NS, 2, DFF])
                self.load(cs1[:], self.s_ua[SEQ:SEQ + 2, :], dram=[self.bua[15]])
                self.out_store(self.o_fconv_p[l], cs1[:])
                self.load(cs2[:, 0, :], self.st_fconv[l, :, 1, :])
                self.load(cs2[:, 1, :], self.s_ua_s[:, :], dram=[self.bua[16]], partial=True)
                self.out_store(self.o_fconv_s[l], cs2[:])
                kb.barrier()

    def ffn_elem(self, l):
        kb = self.kb
        PW = 1408
        with ExitStack() as es:
            cw = [self.sb(es, "fcw%d" % j, [128, 3, PW]) for j in range(2)]
            cb = [self.sb(es, "fcb%d" % j, [128, PW]) for j in range(2)]
            xa = [self.sb(es, "fxa%d" % j, [128, 3, PW]) for j in range(2)]
            xb = [self.sb(es, "fxb%d" % j, [128, PW]) for j in range(2)]
            acc = [self.sb(es, "facc%d" % j, [128, PW]) for j in range(2)]
            tmp = [self.sb(es, "ftmp%d" % j, [128, PW]) for j in range(2)]
            tmp2 = [self.sb(es, "ftmq%d" % j, [128, PW]) for j in range(2)]
            ab = [self.sb(es, "fab%d" % j, [128, PW], BF16) for j in range(2)]
            stg = [self.sb(es, "fstg%d" % j, [128, 11, 128], BF16) for j in range(2)]
            its = [(part, t) for part in range(4) for t in self.TILES]

            def loads(n):
                part, (i, c0, tp) = its[n]
                pc = part * PW
                if i == 0:
                    self.load(cw[part % 2][:], self.ffn_conv_w[l:l + 1, :, pc:pc + PW].partition_broadcast(128))
                    self.load(cb[part % 2][:], self.ffn_conv_b[l:l + 1, pc:pc + PW].partition_broadcast(128))
                a3, b1 = xa[n % 2], xb[n % 2]
                for s_ in range(3):
                    if i < 16:
                        src, dr = self.s_ua[c0 + s_:c0 + s_ + 128, pc:pc + PW], [self.bua[max(i - 1, 0)], self.bua[i]]
                    elif s_ < 2:
                        src, dr = self.st_fconv[l, :, s_, pc:pc + PW], []
                    else:
                        src, dr = self.s_ua_s[:, pc:pc + PW], [self.bua[16]]
                    self.load(a3[0:tp, s_, :], src, dram=dr, partial=(s_ > 0))
                self.load(b1[0:tp, :], self.s_ub[c0:c0 + tp, pc:pc + PW], dram=[self.bub[i]])

            loads(0)
            for n, (part, (i, c0, tp)) in enumerate(its):
                if n + 1 < len(its):
                    loads(n + 1)
                a3, b1, sg = xa[n % 2], xb[n % 2], stg[n % 2]
                ac, t1, t2, abb = acc[n % 2], tmp[n % 2], tmp2[n % 2], ab[n % 2]
                w, bb = cw[part % 2], cb[part % 2]
                self.tt("pool", ac[0:tp, :], a3[0:tp, 0, :], w[0:tp, 0, :], ALU.mult)
                self.tt("dve", t1[0:tp, :], a3[0:tp, 1, :], w[0:tp, 1, :], ALU.mult)
                self.tt("pool", t2[0:tp, :], a3[0:tp, 2, :], w[0:tp, 2, :], ALU.mult)
                self.tt("dve", t1[0:tp, :], t1[0:tp, :], bb[0:tp, :], ALU.add)
                self.tt("dve", ac[0:tp, :], ac[0:tp, :], t1[0:tp, :], ALU.add)
                self.tt("dve", ac[0:tp, :], ac[0:tp, :], t2[0:tp, :], ALU.add)
                self.act(ac[0:tp, :], ac[0:tp, :], AF.Silu)
                self.tt("dve", abb[0:tp, :], ac[0:tp, :], b1[0:tp, :], ALU.mult)
                for (k0, nk) in ((0, 8), (8, 3)):
                    bk = self.bank()
                    pv = bk[:].cast(BF16)
                    self.tr(bk, [(pv[:, k * 128:k * 128 + tp], abb[0:tp, (k0 + k) * 128:(k0 + k + 1) * 128]) for k in range(nk)],
                            self.idb[0:tp, 0:tp])
                    self.cp("act", sg[:, k0:k0 + nk, 0:tp], pv[:, 0:nk * 128].r("p (k c) -> p k c", c=128)[:, :, 0:tp])
                dst = self.s_actT[i].rearrange("p (k c) -> p k c", c=128)[:, part * 11:(part + 1) * 11, 0:tp]
                self.store(dst, sg[:, :, 0:tp], dram=[self.bactT[i]])
            kb.barrier()

    def g5(self, l):
        kb = self.kb
        with ExitStack() as es:
            wsl = [self.sb(es, "g5w%d" % j, [128, 44, 512], BF16) for j in range(2)]
            at = [self.sb(es, "g5a%d" % j, [128, 44, 128], BF16) for j in range(3)]
            pre, evac = self.x_update_evac(es, l, False, 5 * D, "5")
            its = [(j, t) for j in range(4) for t in self.TILES]

            def loads(n):
                j, (i, c0, tp) = its[n]
                a = at[n % 3]
                self.load(a[:, :, 0:tp], self.s_actT[i].rearrange("p (k c) -> p k c", c=128)[:, :, 0:tp], dram=[self.bactT[i]])
                pre((i, c0, tp), (j * 512, 512))

            loads(0)
            for n, (j, (i, c0, tp)) in enumerate(its):
                w = wsl[j % 2]
                if i == 0:
                    self.load(w[:], self.w_ffn_out[l, :, j * 512:(j + 1) * 512].rearrange("(kc p) n -> p kc n", p=128), q="pool")
                if n + 1 < len(its):
                    loads(n + 1)
                a = at[n % 3]
                bk = self.bank()
                self.mm(bk, [(bk[0:tp, :], a[:, k, 0:tp], w[:, k, :], k == 0, k == 43) for k in range(44)])
                evac((i, c0, tp), (j * 512, 512), bk)
            kb.barrier()

    def layer(self, l):
        import os
        mx = os.environ.get("MIX", "ABC")
        with ExitStack() as es:
            XT = self.sb(es, "XT", [128, KC, NTOK], BF16)
            self.norm_phase(l, 1, XT)
            self.g1(l, XT)
        if self.stop == "g1":
            return
        if "A" in mx:
            self.mixA(l)
        if "B" in mx:
            self.mixB(l)
        if "C" in mx:
            self.mixC(l)
        if self.stop == "mix":
            return
        self.g23(l)
        if self.stop == "g3":
            return
        self.g4(l)
        self.ffn_elem(l)
        self.g5(l)

    def final_norm(self):
        kb = self.kb
        with ExitStack() as es:
            gt = self.sb(es, "fng", [128, D])
            xsl = [self.sb(es, "fnx%d" % i, [128, D]) for i in range(2)]
            jk = self.sb(es, "fnj", [128, D], BF16)
            ssq = [self.sb(es, "fnq%d" % i, [128, 2]) for i in range(2)]
            self.load(gt[:], self.final_g[0:1, :].partition_broadcast(128))
            for (i, c0, tp) in self.TILES:
                xt, sq = xsl[i % 2], ssq[i % 2]
                self.load(xt[0:tp, :], self.s_x[c0:c0 + tp, :], dram=[self.bx[i]])
                self.memset("dve", sq[0:tp, :], 0.0)
                self.act(jk[0:tp, :], xt[0:tp, :], AF.Square, accum=sq[0:tp, 0:1])
                self.act(sq[0:tp, 1:2], sq[0:tp, 0:1], AF.Sqrt, bias=EPS, scale=1.0 / D)
                self.recip(sq[0:tp, 1:2], sq[0:tp, 1:2])
                self.stt("dve", xt[0:tp, :], xt[0:tp, :], sq[0:tp, 1:2], gt[0:tp, :], ALU.mult, ALU.mult)
                self.out_store(self.y_p[c0:c0 + tp, :] if i < 16 else self.y_s[:, :], xt[0:tp, :])
            kb.barrier()


_W_KEYS = ["w_ada", "b_ada", "norm1_g", "w_in", "conv_a_w", "a_log", "dt_bias", "norm_a_g", "w_lr2", "b_lr2",
           "norm_c_g", "w_out", "norm2_g", "w_ffn_in", "ffn_conv_w", "ffn_conv_b", "w_ffn_out"]


def _core_inputs(inp, c, shared):
    p = c % 4
    s0 = c * NS
    m = dict(shared)
    m["xp"] = inp["x_prompt"][p]
    m["xs"] = inp["x_sample"][s0:s0 + NS, 0, :]
    m["c17"] = np.concatenate([inp["c_sample"][s0:s0 + NS], inp["c_prompt"][p:p + 1]], 0)
    m["st_delta"] = inp["state_delta"][:, s0:s0 + NS]
    m["st_dconv"] = inp["state_delta_conv"][:, s0:s0 + NS]
    m["st_ret"] = inp["state_ret"][:, s0:s0 + NS]
    m["st_gla"] = inp["state_gla"][:, s0:s0 + NS]
    m["st_fconv"] = inp["state_ffn_conv"][:, s0:s0 + NS]
    return {k: np.ascontiguousarray(v, dtype=np.float32) for k, v in m.items()}


def kernel(**inp):
    inp = {k: np.asarray(v) for k, v in inp.items()}
    cst, _ = _consts()
    rp, rs = _rope_tables()
    shared = {k: np.ascontiguousarray(inp[k], dtype=np.float32) for k in _W_KEYS}
    shared["w_branch"] = np.ascontiguousarray(inp["w_branch"], dtype=np.float32).reshape(DEPTH, 3072, D)
    shared["final_g"] = np.ascontiguousarray(inp["final_norm_g"], dtype=np.float32)[None, :]
    shared["cst"] = cst
    shared["rope_p"] = rp
    shared["rope_s"] = rs
    P = Prog()
    nc = P.build()
    in_maps = [_core_inputs(inp, c, shared) for c in range(8)]
    res = run_bass_kernel_spmd(nc, in_maps, core_ids=list(range(8)))
    r = res.results
    B = 4
    y_p = np.stack([r[c]["y_p"] for c in range(B)], 0)
    y_s = np.concatenate([r[c]["y_s"] for c in range(8)], 0)[:, None, :]

    def pstack(name):
        return np.stack([r[c][name] for c in range(B)], 1)

    def sstack(name):
        return np.concatenate([r[c][name] for c in range(8)], 1)

    outs = (y_p, y_s, pstack("o_delta_p"), pstack("o_dconv_p"), pstack("o_ret_p"), pstack("o_gla_p"), pstack("o_fconv_p"),
            sstack("o_delta_s"), sstack("o_dconv_s"), sstack("o_ret_s"), sstack("o_gla_s"), sstack("o_fconv_s"))
    return tuple(np.ascontiguousarray(o, dtype=np.float32) for o in outs)
```
